# Optimizing a Trainium2 kernel written in Bass

```python
import math
import jax, jax.numpy as jnp
from jax import lax
import numpy as np

D_MODEL = 2048
BATCH = 8
SEQ = 4096
DEPTH = 1
DEC_BATCH = 1
DEC_SEQ = 8192
PAST_LEN = 128

W_A = D_MODEL // 2
W_B = D_MODEL // 2
K_A = 3
K_B = 31
N_MEM = 256
N_XHEADS = 4
XHEAD_DIM = D_MODEL // N_XHEADS
D_FF = int(math.ceil((8 * D_MODEL / 3) / 256) * 256)
EPS = 1e-6
SPLIT_SIZES = (W_A, W_A, W_A, W_B, W_B, D_MODEL, D_MODEL)
SPLIT_POINTS = tuple(int(s) for s in np.cumsum(SPLIT_SIZES)[:-1])
D_IN = int(sum(SPLIT_SIZES))

kernel_name = "hybrid_conv_gated_encoder"


def rmsnorm(x, g):
    xf = x.astype(jnp.float32)
    y = xf * lax.rsqrt(jnp.mean(xf * xf, axis=-1, keepdims=True) + EPS)
    return (y * g.astype(jnp.float32)).astype(x.dtype)


def layernorm(x, g, b):
    xf = x.astype(jnp.float32)
    mu = jnp.mean(xf, axis=-1, keepdims=True)
    var = jnp.mean(jnp.square(xf - mu), axis=-1, keepdims=True)
    y = (xf - mu) * lax.rsqrt(var + EPS)
    return (y * g.astype(jnp.float32) + b.astype(jnp.float32)).astype(x.dtype)


def depthwise_conv(x, w):
    k, c = w.shape
    return lax.conv_general_dilated(
        x, w[:, None, :].astype(x.dtype), window_strides=(1,),
        padding=[(k // 2, k // 2)], dimension_numbers=("NWC", "WIO", "NWC"),
        feature_group_count=c)


def encoder_layer(x, mem, g_mix, w_in, conv_a_w, w_out_a, conv_b_w, conv_b_bias,
                  ln_b_g, ln_b_b, w_out_b, w_o, g_xattn, g_mem, w_q, w_kv, w_xo,
                  g_ffn, w_gate_up, w_down):
    bsz, seq, _ = x.shape
    xn = rmsnorm(x, g_mix)
    proj = xn @ w_in
    a_b, a_c, a_v, b_val, b_gate, gate_a, gate_b = jnp.split(proj, SPLIT_POINTS, axis=-1)
    y_a = (a_b * depthwise_conv(a_c * a_v, conv_a_w)) @ w_out_a
    u = b_val * jax.nn.sigmoid(b_gate)
    u = depthwise_conv(u, conv_b_w) + conv_b_bias
    u = jax.nn.silu(layernorm(u, ln_b_g, ln_b_b))
    y_b = u @ w_out_b
    merged = jax.nn.sigmoid(gate_a) * y_a + jax.nn.sigmoid(gate_b) * y_b
    x = x + merged @ w_o
    hn = rmsnorm(x, g_xattn)
    mn = rmsnorm(mem, g_mem)
    q = (hn @ w_q).reshape(bsz, seq, N_XHEADS, XHEAD_DIM)
    k, v = jnp.split(mn @ w_kv, 2, axis=-1)
    k = k.reshape(bsz, N_MEM, N_XHEADS, XHEAD_DIM)
    v = v.reshape(bsz, N_MEM, N_XHEADS, XHEAD_DIM)
    s = jnp.einsum("bqhd,bkhd->bhqk", q, k).astype(jnp.float32) * (XHEAD_DIM ** -0.5)
    p = jax.nn.softmax(s, axis=-1).astype(v.dtype)
    o = jnp.einsum("bhqk,bkhd->bqhd", p, v).reshape(bsz, seq, D_MODEL)
    x = x + o @ w_xo
    hn = rmsnorm(x, g_ffn)
    gt, up = jnp.split(hn @ w_gate_up, 2, axis=-1)
    x = x + (jax.nn.silu(gt) * up) @ w_down
    return x


def trunk(x, mem, g_mix, w_in, conv_a_w, w_out_a, conv_b_w, conv_b_bias, ln_b_g, ln_b_b,
          w_out_b, w_o, g_xattn, g_mem, w_q, w_kv, w_xo, g_ffn, w_gate_up, w_down, g_final):
    for l in range(DEPTH):
        x = encoder_layer(x, mem, g_mix[l], w_in[l], conv_a_w[l], w_out_a[l], conv_b_w[l],
                          conv_b_bias[l], ln_b_g[l], ln_b_b[l], w_out_b[l], w_o[l],
                          g_xattn[l], g_mem[l], w_q[l], w_kv[l], w_xo[l], g_ffn[l],
                          w_gate_up[l], w_down[l])
    return rmsnorm(x, g_final)


def setup_inputs(seed: int = 0) -> dict:
    key = jax.random.key(seed)
    ks = jax.random.split(key, 24)
    f32 = jnp.float32

    def nrm(k, shape, scale):
        return jax.random.normal(k, shape, f32) * scale

    def gain(k, shape):
        return 1.0 + 0.05 * jax.random.normal(k, shape, f32)

    L = DEPTH
    return {
        "x_prompt": nrm(ks[0], (BATCH, SEQ, D_MODEL), 1.0),
        "x_sample": nrm(ks[1], (DEC_BATCH, DEC_SEQ, D_MODEL), 1.0),
        "mem_prompt": nrm(ks[2], (BATCH, N_MEM, D_MODEL), 1.0),
        "mem_sample": nrm(ks[3], (DEC_BATCH, N_MEM, D_MODEL), 1.0),
        "g_mix": gain(ks[4], (L, D_MODEL)),
        "w_in": nrm(ks[5], (L, D_MODEL, D_IN), D_MODEL ** -0.5),
        "conv_a_w": nrm(ks[6], (L, K_A, W_A), K_A ** -0.5),
        "w_out_a": nrm(ks[7], (L, W_A, D_MODEL), W_A ** -0.5),
        "conv_b_w": nrm(ks[8], (L, K_B, W_B), K_B ** -0.5),
        "conv_b_bias": nrm(ks[9], (L, W_B), 0.02),
        "ln_b_g": gain(ks[10], (L, W_B)),
        "ln_b_b": nrm(ks[11], (L, W_B), 0.02),
        "w_out_b": nrm(ks[12], (L, W_B, D_MODEL), W_B ** -0.5),
        "w_o": nrm(ks[13], (L, D_MODEL, D_MODEL), D_MODEL ** -0.5),
        "g_xattn": gain(ks[14], (L, D_MODEL)),
        "g_mem": gain(ks[15], (L, D_MODEL)),
        "w_q": nrm(ks[16], (L, D_MODEL, D_MODEL), D_MODEL ** -0.5),
        "w_kv": nrm(ks[17], (L, D_MODEL, 2 * D_MODEL), D_MODEL ** -0.5),
        "w_xo": nrm(ks[18], (L, D_MODEL, D_MODEL), D_MODEL ** -0.5),
        "g_ffn": gain(ks[19], (L, D_MODEL)),
        "w_gate_up": nrm(ks[20], (L, D_MODEL, 2 * D_FF), D_MODEL ** -0.5),
        "w_down": nrm(ks[21], (L, D_FF, D_MODEL), D_FF ** -0.5),
        "g_final": gain(ks[22], (D_MODEL,)),
    }


def reference(x_prompt, x_sample, mem_prompt, mem_sample, g_mix, w_in, conv_a_w, w_out_a,
              conv_b_w, conv_b_bias, ln_b_g, ln_b_b, w_out_b, w_o, g_xattn, g_mem, w_q,
              w_kv, w_xo, g_ffn, w_gate_up, w_down, g_final):
    y_prompt = trunk(x_prompt, mem_prompt, g_mix, w_in, conv_a_w, w_out_a, conv_b_w,
                     conv_b_bias, ln_b_g, ln_b_b, w_out_b, w_o, g_xattn, g_mem, w_q, w_kv,
                     w_xo, g_ffn, w_gate_up, w_down, g_final)
    y_sample = trunk(x_sample, mem_sample, g_mix, w_in, conv_a_w, w_out_a, conv_b_w,
                     conv_b_bias, ln_b_g, ln_b_b, w_out_b, w_o, g_xattn, g_mem, w_q, w_kv,
                     w_xo, g_ffn, w_gate_up, w_down, g_final)
    return (y_prompt, y_sample)
```

```python
import contextlib
import numpy as np
import ml_dtypes
import concourse.bass as bass
import concourse.mybir as mybir
from concourse.bass_utils import run_bass_kernel_spmd

F32 = mybir.dt.float32
BF16 = mybir.dt.bfloat16
AF = mybir.ActivationFunctionType
ALU = mybir.AluOpType
AX = mybir.AxisListType

D = 2048
KD = 16
DIN = 9216
WC = 8
DFF = 5632
KF = 44
NMEM = 256
NH = 4
TT = 512
NG = 4
HALO = 16
TW = TT + 2 * HALO
HW2 = TW // 2
EPS = 1e-6
SCALE = 512.0 ** -0.5
NSLOT = 4
SLOT_ELEMS = 4096
NPRE_SEM = 40

C_GMIX, C_GX, C_GF, C_GMEM = 0, 16, 32, 48
C_CA = 64
C_CB = 88
C_BIAS = 336
C_LNG = 344
C_LNB = 352
C_EPS = 360
C_GFIN = 368
NCST = 368 + 2048


class Op:
    __slots__ = ("eng", "emit", "deps", "signal", "count", "dma", "semkey", "semval", "idx", "pos")

    def __init__(self, eng, emit, dma=False, semkey=None):
        self.eng = eng
        self.emit = emit
        self.deps = {}
        self.signal = False
        self.count = 0
        self.dma = dma
        self.semkey = semkey
        self.semval = 0
        self.idx = 0


class Region:
    __slots__ = ("lo", "hi", "writer", "readers")

    def __init__(self, lo, hi, writer):
        self.lo, self.hi, self.writer = lo, hi, writer
        self.readers = {}


class Sched:
    ENGS = ("pe", "act", "dve", "pool", "sp")

    def __init__(self):
        self.ops = {e: [] for e in self.ENGS}
        self.spaces = {}
        self.dma_counts = {}
        self.seq = 0
        self.ins = 0

    def _add_dep(self, op, dep):
        if dep is None or dep is op:
            return
        if (not dep.dma) and dep.eng == op.eng and op.eng == "pe":
            return
        key = ("d", dep.semkey) if dep.dma else ("e", dep.eng)
        cur = op.deps.get(key)
        if cur is None or dep.idx > cur.idx:
            op.deps[key] = dep

    def _access(self, op, res, write):
        space, lo, hi = res
        regs = self.spaces.setdefault(space, [])
        for r in regs:
            if r.lo < hi and lo < r.hi:
                self._add_dep(op, r.writer)
                if write:
                    for o in r.readers.values():
                        self._add_dep(op, o)
        if write:
            regs[:] = [r for r in regs if not (lo <= r.lo and r.hi <= hi)]
            regs.append(Region(lo, hi, op))
        else:
            for r in regs:
                if r.lo == lo and r.hi == hi:
                    break
            else:
                r = Region(lo, hi, None)
                regs.append(r)
            key = ("d", op.semkey, op.idx) if op.dma else op.eng
            r.readers[key] = op

    def op(self, eng, emit, reads=(), writes=(), dma=False, semkey=None, at=None):
        o = Op(eng, emit, dma, semkey)
        self.seq += 1
        o.pos = self.seq
        if dma:
            n = self.dma_counts.get(semkey, 0) + 1
            self.dma_counts[semkey] = n
            o.semval = 16 * n
            o.idx = n
        else:
            o.idx = o.pos
        for r in reads:
            self._access(o, r, False)
        for r in writes:
            self._access(o, r, True)
        if at == "deps" and o.deps:
            self.ins += 1
            o.pos = max(d.pos for d in o.deps.values()) + 0.5 + 1e-6 * (self.ins % 1000)
            if not dma:
                o.idx = o.pos
        self.ops[eng].append(o)
        return o

    def emit_all(self, nc, stack):
        for e in self.ENGS:
            self.ops[e].sort(key=lambda o: o.pos)
        for e in self.ENGS:
            for o in self.ops[e]:
                for d in o.deps.values():
                    d.signal = True
        esem = {}
        for e in ("pe", "act", "dve", "pool"):
            esem[e] = stack.enter_context(nc.semaphore("sem_" + e))
            c = 0
            for o in self.ops[e]:
                if not o.dma and o.signal:
                    c += 1
                    o.count = c
        dsem = {}
        for k in self.dma_counts:
            dsem[k] = stack.enter_context(nc.semaphore("dsem_%s" % (k,)))
        finals = [(dsem[k], 16 * n) for k, n in self.dma_counts.items()]

        def run(ename, e):
            waited = {}
            for o in self.ops[ename]:
                for d in o.deps.values():
                    if d.dma:
                        s, v, key = dsem[d.semkey], d.semval, ("d", d.semkey)
                    else:
                        s, v, key = esem[d.eng], d.count, ("e", d.eng)
                    if waited.get(key, 0) >= v:
                        continue
                    waited[key] = v
                    e.wait_ge(s, v)
                ins = o.emit(e)
                if o.dma:
                    ins.then_inc(dsem[o.semkey], 16)
                elif o.signal:
                    ins.then_inc(esem[ename], 1)
            if ename == "sp":
                for s, v in finals:
                    e.wait_ge(s, v)

        with nc.Block() as block:
            @block.tensor
            def _(e):
                run("pe", e)

            @block.scalar
            def _(e):
                run("act", e)

            @block.vector
            def _(e):
                run("dve", e)

            @block.gpsimd
            def _(e):
                run("pool", e)

            @block.sync
            def _(e):
                run("sp", e)


def build_program(NPT=8, NST=2, debug=False):
    NT = NPT + NST
    nc = bass.Bass("TRN2", target_bir_lowering=False)
    S = Sched()
    stack = contextlib.ExitStack()

    def din(name, shape, dt=F32):
        return nc.dram_tensor(name, list(shape), dt, kind="ExternalInput").ap()

    xc = din("xc", [NT, TT, D])
    xh = din("xh", [NT, 32, D])
    mem = din("mem", [2, NMEM, D])
    cst_d = din("cst", [128, NCST])
    idn_d = din("idn", [128, 256], BF16)
    W = {
        "w_in": din("w_in", [D, DIN]),
        "w_out_a": din("w_out_a", [1024, D]),
        "w_out_b": din("w_out_b", [1024, D]),
        "w_o": din("w_o", [D, D]),
        "w_q": din("w_q", [D, D]),
        "w_kv": din("w_kv", [D, 2 * D]),
        "w_xo": din("w_xo", [D, D]),
        "w_gate_up": din("w_gate_up", [D, 2 * DFF]),
        "w_down": din("w_down", [DFF, D]),
    }
    y_d = nc.dram_tensor("y", [NT, TT, D], F32, kind="ExternalOutput").ap()
    dbg_i = [0]

    def dump(name, ap, shape, dt, reads):
        if not debug:
            return
        t = nc.dram_tensor("dbg_" + name, list(shape), dt, kind="ExternalOutput").ap()
        S.op("pool", lambda e: e.dma_start(out=t, in_=ap), reads=reads, dma=True, semkey="dbg%d" % dbg_i[0])
        dbg_i[0] += 1

    blocks = []

    def addblk(segs):
        nk = segs[0][2]
        tc = sum(s[4] for s in segs)
        assert nk * tc <= SLOT_ELEMS
        blocks.append((segs, nk, tc))
        return len(blocks) - 1

    B_k = [addblk([("w_kv", 0, 16, c * 256, 256)]) for c in range(8)]
    B_v = [[addblk([("w_kv", h * 1024, 8, 2048 + cb * 512, 512)]) for h in range(2)] for cb in range(4)]
    B_in = [addblk([("w_in", 0, 16, c * 256, 256)]) for c in range(36)]
    B_oa = [addblk([("w_out_a", 0, 8, c * 256, 256)]) for c in range(8)]
    B_ob = [addblk([("w_out_b", 0, 8, c * 256, 256)]) for c in range(8)]
    B_o = [[addblk([("w_o", h * 1024, 8, cb * 512, 512)]) for h in range(2)] for cb in range(4)]
    B_q = [addblk([("w_q", 0, 16, c * 256, 256)]) for c in range(8)]
    B_xo = [[addblk([("w_xo", h * 1024, 8, cb * 512, 512)]) for h in range(2)] for cb in range(4)]
    B_gg = [addblk([("w_gate_up", 0, 16, f2 * 256, 256)]) for f2 in range(KF // 2)]
    B_uu = [addblk([("w_gate_up", 0, 16, DFF + f2 * 256, 256)]) for f2 in range(KF // 2)]
    DN_P = [(0, 8), (8, 8), (16, 8), (24, 8), (32, 8), (40, 4)]
    B_dn = [[addblk([("w_down", k0 * 128, nk, cb * 512, 512)]) for (k0, nk) in DN_P] for cb in range(4)]
    NBLK = len(blocks)
    wscr = nc.dram_tensor("wscr", [NBLK, 128, SLOT_ELEMS], BF16, kind="Internal").ap()

    def sb(name, shape, dt):
        return stack.enter_context(nc.sbuf_tensor(name, list(shape), dt))

    xr = sb("xr", [128, NG, D], F32)
    stage = sb("stage", [128, 2, D], F32)
    xs = sb("xs", [128, 2, D], BF16)
    junk = sb("junk", [128, D], BF16)
    xnT = sb("xnT", [128, KD, TW], BF16)
    ARENA_E = 28672
    arena = sb("arena", [128, ARENA_E], BF16)
    wring = sb("wring", [128, NSLOT, SLOT_ELEMS], BF16)
    kT = sb("kT", [128, KD, NMEM], BF16)
    vm = sb("vm", [128, 2, D], BF16)
    cst = sb("cst_sb", [128, NCST], F32)
    idn = sb("idn_sb", [128, 256], BF16)
    dgA = sb("dgA", [128, 24, 128], BF16)
    dgB = sb("dgB", [128, 31, 128], BF16)
    small = sb("small", [128, 256], F32)
    small_i = small[:].bitcast(mybir.dt.int32)
    psf = stack.enter_context(nc.psum_tensor("psf", [128, 8, 512], F32))
    psb = psf[:].bitcast(BF16)
    ident = idn[:, 0:128]
    ones = idn[:, 128:256]

    class AView:
        def __init__(self, off_e, n_e, dt, shape=None):
            self.res = ("arena", off_e, off_e + n_e)
            ap = arena[:, off_e:off_e + n_e]
            if dt == F32:
                ap = ap.bitcast(F32)
            self.ap = ap
            self.n = n_e if dt == BF16 else n_e // 2

    A_ZA = 0
    A_V = 4096
    A_ZB = 12288
    A_MG = 16384
    A_T = 24576
    A_Q = 0
    A_OT = 8192
    A_P = 16384
    A_PN = 18432
    A_PT = 19456
    A_H = 0
    A_SG = 22528
    A_MN = 0

    def av(off, n, dt=BF16):
        return AView(off, n, dt)

    def R(space, lo, hi=None):
        return (space, lo, lo + 1 if hi is None else hi)

    def PS(b, n=1):
        return ("ps", b, b + n)

    def cc(col, n=1):
        return cst[:, col:col + n]

    small_ptr = [0]

    def scol(n=1):
        if small_ptr[0] + n > 250:
            small_ptr[0] = 0
        c = small_ptr[0]
        small_ptr[0] += n
        return c

    def SM(c, n=1):
        return ("small", c, c + n)

    def act(out, in_, func, reads, writes, bias=None, scale=None, accum_out=None):
        kw = {}
        if bias is not None:
            kw["bias"] = bias
        if scale is not None:
            kw["scale"] = scale
        if accum_out is not None:
            kw["accum_out"] = accum_out
        return S.op("act", lambda e: e.activation(out, in_, func, **kw), reads, writes)

    def dve_tt(out, in0, in1, op, reads, writes, eng="dve"):
        return S.op(eng, lambda e: e.tensor_tensor(out, in0, in1, op), reads, writes)

    def ts(eng, out, in0, s1, s2, op0, op1, reads, writes):
        if op1 is None:
            return S.op(eng, lambda e: e.tensor_scalar(out, in0, s1, None, op0), reads, writes)
        return S.op(eng, lambda e: e.tensor_scalar(out, in0, s1, s2, op0, op1), reads, writes)

    def mm(out, lhsT, rhs, start, stop, reads, writes):
        return S.op("pe", lambda e: e.matmul(out, lhsT, rhs, start=start, stop=stop), reads, writes)

    def tr(out, in_, idm, reads, writes):
        return S.op("pe", lambda e: e.transpose(out, in_, idm), reads, writes)


    ws_seq = [0]
    ws_mark = []
    ws_done = set()
    ws_uses = {}
    NKVBLK = 16
    PF = NSLOT - 1

    def wload(b):
        i = ws_seq[0]
        slot = i % NSLOT
        ws_seq[0] += 1
        at = "deps"
        segs, nk, tc = blocks[b]
        n = nk * tc
        if b in ws_done:
            S.op("sp", lambda e: e.dma_start(out=wring[:, slot, 0:n], in_=wscr[b][:, 0:n]),
                 reads=[R("wscr", b)], writes=[R("wring", slot)], dma=True, semkey="w%d" % slot, at=at)
        else:
            ws_uses[b] = ws_uses.get(b, 0) + 1
            do_store = True
            if do_store:
                ws_done.add(b)
            off = 0
            for si, (wn, row0, nk_, col0, ncols) in enumerate(segs):
                src = W[wn][row0:row0 + nk * 128, col0:col0 + ncols].rearrange("(k p) c -> p k c", p=128)
                dst = wring[:, slot, 0:n].rearrange("p (k c) -> p k c", k=nk)[:, :, off:off + ncols]
                S.op("pool", lambda e, src=src, dst=dst: e.dma_start(out=dst, in_=src),
                     writes=[R("wring", slot)],
                     dma=True, semkey="w%d" % slot, at=at)
                off += ncols
            if do_store:
                S.op("sp", lambda e: e.dma_start(out=wscr[b][:, 0:n], in_=wring[:, slot, 0:n]),
                     reads=[R("wring", slot)], writes=[R("wscr", b)], dma=True, semkey="ws%d" % slot)
        return slot

    def wv(slot, nk, tc):
        return wring[:, slot, 0:nk * tc].rearrange("p (k c) -> p k c", k=nk)

    S.op("sp", lambda e: e.dma_start(out=cst[:], in_=cst_d), writes=[R("cst", 0)], dma=True, semkey="c0")
    S.op("sp", lambda e: e.dma_start(out=idn[:], in_=idn_d), writes=[R("idn", 0)], dma=True, semkey="c1")

    for j in range(WC):
        for k in range(3):
            i = j * 3 + k
            ts("pool", dgA[:, i, :], ident, cc(C_CA + i), 0.0, ALU.mult, ALU.add,
               reads=[R("cst", 0), R("idn", 0)], writes=[R("dgA", i)])

    stage_ptr = [0]

    def rstd_cols(c_ss, n, nrows=128, iters=3):
        c_x, c_t, c_y, c_a = scol(n), scol(n), scol(n), scol(n)
        P_ = slice(0, nrows)

        def f(c):
            return small[P_, c:c + n]

        def fi(c):
            return small_i[P_, c:c + n]
        ts("dve", f(c_x), f(c_ss), 1.0 / D, EPS, ALU.mult, ALU.add, reads=[SM(c_ss, n)], writes=[SM(c_x, n)])
        S.op("dve", lambda e: e.tensor_scalar(fi(c_t), fi(c_x), 1, None, ALU.logical_shift_right),
             reads=[SM(c_x, n)], writes=[SM(c_t, n)])
        S.op("dve", lambda e: e.tensor_scalar(fi(c_y), fi(c_t), -1.0, 1597463007.0, ALU.mult, ALU.add),
             reads=[SM(c_t, n)], writes=[SM(c_y, n)])
        for it in range(iters):
            dve_tt(f(c_a), f(c_y), f(c_y), ALU.mult, reads=[SM(c_y, n)], writes=[SM(c_a, n)])
            dve_tt(f(c_a), f(c_a), f(c_x), ALU.mult, reads=[SM(c_x, n)], writes=[SM(c_a, n)])
            ts("dve", f(c_a), f(c_a), -0.5, 1.5, ALU.mult, ALU.add, reads=[], writes=[SM(c_a, n)])
            dve_tt(f(c_y), f(c_y), f(c_a), ALU.mult, reads=[SM(c_a, n)], writes=[SM(c_y, n)])
        return c_y

    def rms_stats(src_ap, src_res, nrows=128):
        c_ss = scol()
        act(junk[0:nrows, :], src_ap, AF.Square, reads=[src_res], writes=[SM(c_ss)],
            accum_out=small[0:nrows, c_ss:c_ss + 1])
        return rstd_cols(c_ss, 1, nrows)

    def rstd_from_parts4(c4, iters=2):
        c_ss = scol()
        S.op("dve", lambda e: e.tensor_reduce(small[:, c_ss:c_ss + 1], small[:, c4:c4 + 4], AX.X, ALU.add),
             reads=[SM(c4, 4)], writes=[SM(c_ss)])
        return rstd_cols(c_ss, 1, iters=iters)

    def rstd_from_parts8(c8, iters=2):
        c_ss = scol(2)
        S.op("dve", lambda e: e.tensor_reduce(small[:, c_ss:c_ss + 2],
                                              small[:, c8:c8 + 8].rearrange("p (g c) -> p g c", g=2),
                                              AX.X, ALU.add),
             reads=[SM(c8, 8)], writes=[SM(c_ss, 2)])
        return rstd_cols(c_ss, 2, iters=iters)

    def rstd_from_parts16(c16):
        c_ss = scol(4)
        S.op("dve", lambda e: e.tensor_reduce(small[:, c_ss:c_ss + 4],
                                              small[:, c16:c16 + 16].rearrange("p (g c) -> p g c", g=4),
                                              AX.X, ALU.add),
             reads=[SM(c16, 16)], writes=[SM(c_ss, 4)])
        return rstd_cols(c_ss, 4)

    tp_ptr = [0]
    xs_ptr = [0]

    def nt_part1(src_ap, src_res, nrows, c_r=None):
        if c_r is None:
            c_r = rms_stats(src_ap, src_res, nrows)
        xb = xs_ptr[0] % 2
        xs_ptr[0] += 1
        HD2 = D // 2
        act(xs[0:nrows, xb, 0:HD2], src_ap[:, 0:HD2], AF.Copy, reads=[src_res, SM(c_r)],
            writes=[R("xs", xb * 2)], scale=small[0:nrows, c_r:c_r + 1])
        ts("dve", xs[0:nrows, xb, HD2:D], src_ap[:, HD2:D], small[0:nrows, c_r:c_r + 1], None, ALU.mult, None,
           reads=[src_res, SM(c_r)], writes=[R("xs", xb * 2 + 1)])
        return (xb, nrows)

    def nt_tr(st, pairs=(4, 6)):
        xb, nrows = st
        pb = pairs[tp_ptr[0] % len(pairs)]
        tp_ptr[0] += 1
        for kc in range(KD):
            bank = pb + kc // 8
            o = (kc % 8) * 128
            tr(psb[:, bank, o:o + nrows], xs[0:nrows, xb, kc * 128:(kc + 1) * 128], ident[0:nrows, 0:nrows],
               reads=[R("xs", xb * 2 + kc // 8), R("idn", 0)], writes=[PS(bank)])
        return pb

    def nt_ev(pb, gcol, dstT, dst_res_fn, col_fn):
        order = [(k, "act") for k in range(0, 6)] + [(k, "dve") for k in range(8, 16)] + \
                [(k, "dve") for k in range(6, 8)]
        for kc, eng in order:
            bank = pb + kc // 8
            o = (kc % 8) * 128
            for (r0, n, dc) in col_fn():
                src = psb[:, bank, o + r0:o + r0 + n]
                dst = dstT[:, kc, dc:dc + n]
                if eng == "act":
                    act(dst, src, AF.Copy, reads=[R("cst", 0)], writes=[PS(bank), dst_res_fn(kc)],
                        scale=cc(gcol + kc))
                else:
                    ts("dve", dst, src, cc(gcol + kc), None, ALU.mult, None,
                       reads=[R("cst", 0)], writes=[PS(bank), dst_res_fn(kc)])

    def nt_part2(st, gcol, dstT, dst_res_fn, col_fn, pairs=(4, 6)):
        pb = nt_tr(st, pairs)
        nt_ev(pb, gcol, dstT, dst_res_fn, col_fn)

    def norm_transpose(src_ap, src_res, nrows, gcol, dstT, dst_res_fn, col_fn, c_r=None, pairs=(4, 6)):
        st = nt_part1(src_ap, src_res, nrows, c_r)
        nt_part2(st, gcol, dstT, dst_res_fn, col_fn, pairs)

    def load_stage(src_dram, nrows, queue="pool"):
        s = stage_ptr[0] % 2
        stage_ptr[0] += 1
        S.op(queue, lambda e: e.dma_start(out=stage[0:nrows, s, :], in_=src_dram),
             writes=[R("stage", s)], dma=True, semkey="ld%d" % s)
        return s

    def prologue_p1(ti, grp, queue="pool"):
        if grp < 0:
            s = load_stage(xh[ti], 32, queue)
            return (grp, nt_part1(stage[0:32, s, :], R("stage", s), 32))
        s = load_stage(xc[ti, grp * 128:(grp + 1) * 128, :], 128, queue)
        return (grp, nt_part1(stage[:, s, :], R("stage", s), 128))

    def prologue_p2(st):
        grp, st1 = st
        if grp < 0:
            nt_part2(st1, C_GMIX, xnT, lambda kc: R("xnT", kc), lambda: [(0, 16, 0), (16, 16, HALO + TT)])
        else:
            nt_part2(st1, C_GMIX, xnT, lambda kc: R("xnT", kc), lambda: [(0, 128, HALO + grp * 128)])

    def prologue_group(ti, grp):
        prologue_p2(prologue_p1(ti, grp))

    kvscr_k = nc.dram_tensor("kvscr_k", [128, KD * NMEM], BF16, kind="Internal").ap()
    kvscr_v = nc.dram_tensor("kvscr_v", [128, 2 * D], BF16, kind="Internal").ap()

    def kv_phase_both():
        mnTs = [av(A_MN, 4096), av(A_MN + 4096, 4096)]
        mn3s = [m.ap.rearrange("p (k c) -> p k c", k=KD) for m in mnTs]
        k1 = av(A_MN + 8192, 4096)
        v1 = av(A_MN + 12288, 4096)
        k13 = k1.ap.rearrange("p (k c) -> p k c", k=KD)
        v13 = v1.ap.rearrange("p (t c) -> p t c", t=2)

        class _T:
            def __init__(self, m):
                self.m = m

            def __getitem__(self, idx):
                return self.m[idx]
        for ms in range(2):
            for g2 in range(2):
                s = load_stage(mem[ms, g2 * 128:(g2 + 1) * 128, :], 128, "sp")
                norm_transpose(stage[:, s, :], R("stage", s), 128, C_GMEM, _T(mn3s[ms]),
                               lambda kc, ms=ms: mnTs[ms].res, lambda g2=g2: [(0, 128, g2 * 128)])
        bank_i = 0
        for c in range(8):
            slot = wload(B_k[c])
            wvv = wv(slot, 16, 256)
            for ocl in range(2):
                oc = c * 2 + ocl
                for ms in range(2):
                    bank = bank_i % 4
                    bank_i += 1
                    for kc in range(KD):
                        mm(psf[:, bank, 0:NMEM], wvv[:, kc, ocl * 128:(ocl + 1) * 128], mn3s[ms][:, kc, :],
                           kc == 0, kc == KD - 1, reads=[R("wring", slot), mnTs[ms].res], writes=[PS(bank)])
                    dst = kT[:, oc, :] if ms == 0 else k13[:, oc, :]
                    wres = R("kT", oc) if ms == 0 else k1.res
                    if (oc + ms) % 2 == 0:
                        act(dst, psf[:, bank, 0:NMEM], AF.Copy, reads=[], writes=[PS(bank), wres])
                    else:
                        S.op("dve", lambda e, dst=dst, bank=bank: e.tensor_copy(dst, psf[:, bank, 0:NMEM]),
                             writes=[PS(bank), wres])
        for cb in range(4):
            base = (cb % 2) * 4
            for h in range(2):
                slot = wload(B_v[cb][h])
                wvv = wv(slot, 8, 512)
                for ms in range(2):
                    for g2 in range(2):
                        bank = base + ms * 2 + g2
                        for kl in range(8):
                            kc = h * 8 + kl
                            mm(psf[:, bank, :], mn3s[ms][:, kc, g2 * 128:(g2 + 1) * 128], wvv[:, kl, :],
                               kc == 0, kc == KD - 1, reads=[R("wring", slot), mnTs[ms].res], writes=[PS(bank)])
            for ms in range(2):
                for g2 in range(2):
                    bank = base + ms * 2 + g2
                    if ms == 0:
                        dst, wres = vm[:, g2, cb * 512:(cb + 1) * 512], R("vm", g2 * 4 + cb)
                    else:
                        dst, wres = v13[:, g2, cb * 512:(cb + 1) * 512], v1.res
                    if g2 == 0:
                        act(dst, psf[:, bank, :], AF.Copy, reads=[], writes=[PS(bank), wres])
                    else:
                        S.op("dve", lambda e, dst=dst, bank=bank: e.tensor_copy(dst, psf[:, bank, :]),
                             writes=[PS(bank), wres])
        S.op("sp", lambda e: e.dma_start(out=kvscr_k, in_=k1.ap), reads=[k1.res], writes=[R("kvscr", 0)],
             dma=True, semkey="kvs0")
        S.op("sp", lambda e: e.dma_start(out=kvscr_v, in_=v1.ap), reads=[v1.res], writes=[R("kvscr", 1)],
             dma=True, semkey="kvs1")

    def kv_reload():
        S.op("pool", lambda e: e.dma_start(out=kT[:].rearrange("p k c -> p (k c)"), in_=kvscr_k),
             reads=[R("kvscr", 0)], writes=[("kT", 0, 16)], dma=True, semkey="kvl0")
        S.op("pool", lambda e: e.dma_start(out=vm[:].rearrange("p t c -> p (t c)"), in_=kvscr_v),
             reads=[R("kvscr", 1)], writes=[("vm", 0, 8)], dma=True, semkey="kvl1")

    class Rot:
        def __init__(self, banks):
            self.banks = list(banks)
            self.i = 0

        def single(self):
            b = self.banks[self.i % len(self.banks)]
            self.i += 1
            return b

        def pair(self):
            while self.banks[self.i % len(self.banks)] % 2 != 0:
                self.i += 1
            b = self.banks[self.i % len(self.banks)]
            self.i += 2
            return b

    def xn_pair_view(kc):
        return xnT[:, kc, :].rearrange("p (a c) -> p a c", a=2)

    def inproj_pair(slot, ocl, bank):
        wvv = wv(slot, 16, 256)
        for half in range(2):
            for kc in range(KD):
                mm(psf[:, bank + half, 0:HW2], wvv[:, kc, ocl * 128:(ocl + 1) * 128],
                   xnT[:, kc, half * HW2:(half + 1) * HW2], kc == 0, kc == KD - 1,
                   reads=[R("wring", slot), R("xnT", kc)], writes=[PS(bank + half)])

    def inproj_single(slot, ocl, bank):
        wvv = wv(slot, 16, 256)
        for kc in range(KD):
            mm(psf[:, bank, :], wvv[:, kc, ocl * 128:(ocl + 1) * 128], xnT[:, kc, HALO:HALO + TT],
               kc == 0, kc == KD - 1, reads=[R("wring", slot), R("xnT", kc)], writes=[PS(bank)])

    def ps2(bank):
        return psf[:, bank:bank + 2, 0:HW2]

    def v2(ap):
        return ap.rearrange("p (a c) -> p a c", a=2)

    def phase1(ti, finals):
        za = av(A_ZA, 4096)
        vbuf = av(A_V, 8192, F32)
        zb = av(A_ZB, 4096)
        mg = av(A_MG, 8192)
        za3 = za.ap.rearrange("p (k c) -> p k c", k=WC)
        v3 = vbuf.ap.rearrange("p (k c) -> p k c", k=WC)
        zb3 = zb.ap.rearrange("p (k c) -> p k c", k=WC)
        mg3 = mg.ap.rearrange("p (k c) -> p k c", k=KD)

        def zres(base, j, n=512):
            return ("arena", base + j * n, base + (j + 1) * n)

        acs = av(A_T, 1088, F32)
        tts = [av(A_T + 1088, 544), av(A_T + 1632, 544)]
        cA = av(A_T + 2176, 1024, F32)
        rot = Rot(range(8))
        pendA = []

        def conv_a(j, bb):
            tt_ = tts[j % 2]
            bo = rot.single()
            for k in range(3):
                mm(psf[:, bo, :], dgA[:, j * 3 + k, :], tt_.ap[:, HALO - 1 + k:HALO - 1 + k + TT],
                   k == 0, k == 2, reads=[R("dgA", j * 3 + k), tt_.res], writes=[PS(bo)])
            act(cA.ap, psf[:, bo, :], AF.Copy, reads=[], writes=[PS(bo), cA.res])
            dve_tt(za3[:, j, :], psf[:, bb, :], cA.ap, ALU.mult, reads=[cA.res],
                   writes=[PS(bb), zres(A_ZA, j)])

        for jp in range(4):
            s_c = wload(B_in[4 + jp])
            s_v = wload(B_in[8 + jp])
            s_b = wload(B_in[0 + jp])
            for jl in range(2):
                j = jp * 2 + jl
                tt_ = tts[j % 2]
                bc = rot.pair()
                inproj_pair(s_c, jl, bc)
                act(v2(acs.ap), ps2(bc), AF.Copy, reads=[], writes=[PS(bc, 2), acs.res])
                bv = rot.pair()
                inproj_pair(s_v, jl, bv)
                dve_tt(v2(tt_.ap), ps2(bv), v2(acs.ap), ALU.mult, reads=[acs.res], writes=[PS(bv, 2), tt_.res])
                if pendA:
                    conv_a(*pendA.pop())
                bb = rot.single()
                inproj_single(s_b, jl, bb)
                pendA.append((j, bb))
            if finals:
                finals[jp]()
        conv_a(*pendA.pop())
        S.op("pool", lambda e: e.dma_start(out=xr[:], in_=xc[ti].rearrange("(g p) d -> p g d", p=128)),
             writes=[("xr", 0, 16)], dma=True, semkey="xr")

        sg = av(A_T, 1088, F32)
        us = [av(A_T + 1088, 544), av(A_T + 1632, 544)]
        vbs = [av(A_T + 2176, 512), av(A_ZB, 512)]
        vsqs = [av(A_T + 2688, 512), av(A_ZB + 512, 512)]
        rot = Rot(range(6))
        BS1, BS2 = 6, 7
        pendB = []

        def build_dgb(j):
            for k in range(31):
                if ti == 0:
                    act(dgB[:, k, :], ident, AF.Copy, reads=[R("cst", 0), R("idn", 0)], writes=[R("dgB", k)],
                        scale=cc(C_CB + j * 31 + k))
                else:
                    ts("pool", dgB[:, k, :], ident, cc(C_CB + j * 31 + k), 0.0, ALU.mult, ALU.add,
                       reads=[R("cst", 0), R("idn", 0)], writes=[R("dgB", k)])

        def conv_b(j):
            u_ = us[j % 2]
            vres = ("arena", A_V + j * 1024, A_V + (j + 1) * 1024)
            bo = rot.single()
            for k in range(31):
                mm(psf[:, bo, :], dgB[:, k, :], u_.ap[:, 1 + k:1 + k + TT], k == 0, k == 30,
                   reads=[R("dgB", k), u_.res], writes=[PS(bo)])
            if j + 1 < WC:
                build_dgb(j + 1)
            act(v3[:, j, :], psf[:, bo, :], AF.Identity, reads=[R("cst", 0)],
                writes=[PS(bo), vres], bias=cc(C_BIAS + j))
            vb, vsq = vbs[j % 2], vsqs[j % 2]
            S.op("dve", lambda e: e.tensor_copy(vb.ap, v3[:, j, :]), reads=[vres], writes=[vb.res])
            act(vsq.ap, v3[:, j, :], AF.Square, reads=[vres], writes=[vsq.res])
            if j > 0:
                stats_b(j - 1)

        def stats_b(j):
            vb, vsq = vbs[j % 2], vsqs[j % 2]
            mm(psf[:, BS1, :], ones, vb.ap, j == 0, j == WC - 1, reads=[R("idn", 0), vb.res], writes=[PS(BS1)])
            mm(psf[:, BS2, :], ones, vsq.ap, j == 0, j == WC - 1, reads=[R("idn", 0), vsq.res], writes=[PS(BS2)])

        build_dgb(0)
        for jp in range(4):
            s_g = wload(B_in[16 + jp])
            s_v = wload(B_in[12 + jp])
            for jl in range(2):
                j = jp * 2 + jl
                u_ = us[j % 2]
                bg = rot.pair()
                inproj_pair(s_g, jl, bg)
                act(v2(sg.ap), ps2(bg), AF.Sigmoid, reads=[], writes=[PS(bg, 2), sg.res])
                bv = rot.pair()
                inproj_pair(s_v, jl, bv)
                dve_tt(v2(u_.ap), ps2(bv), v2(sg.ap), ALU.mult, reads=[sg.res], writes=[PS(bv, 2), u_.res])
                if pendB:
                    conv_b(pendB.pop())
                pendB.append(j)
        conv_b(pendB.pop())
        stats_b(WC - 1)

        mean = av(A_T, 1024, F32)
        rstd = av(A_T + 1024, 1024, F32)
        tmp = av(A_T + 2048, 1024, F32)
        dd = av(A_T + 3072, 1024, F32)
        ts("dve", mean.ap, psf[:, BS1, :], 1.0 / 1024, None, ALU.mult, None, reads=[], writes=[PS(BS1), mean.res])
        dve_tt(tmp.ap, mean.ap, mean.ap, ALU.mult, reads=[mean.res], writes=[tmp.res])
        S.op("dve", lambda e: e.scalar_tensor_tensor(rstd.ap, psf[:, BS2, :], 1.0 / 1024, tmp.ap,
                                                      ALU.mult, ALU.subtract),
             reads=[tmp.res], writes=[PS(BS2), rstd.res])
        act(tmp.ap, rstd.ap, AF.Sqrt, reads=[rstd.res], writes=[tmp.res], bias=cc(C_EPS))
        S.op("dve", lambda e: e.reciprocal(rstd.ap, tmp.ap), reads=[tmp.res], writes=[rstd.res])
        dds = [dd, tmp]
        for j in range(WC):
            vres = ("arena", A_V + j * 1024, A_V + (j + 1) * 1024)
            d_ = dds[j % 2]
            eng = "dve" if (j % 2 == 0 or ti == 0) else "pool"
            dve_tt(d_.ap, v3[:, j, :], mean.ap, ALU.subtract, reads=[vres, mean.res], writes=[d_.res], eng=eng)
            dve_tt(d_.ap, d_.ap, rstd.ap, ALU.mult, reads=[rstd.res], writes=[d_.res], eng=eng)
            act(zb3[:, j, :], d_.ap, AF.Silu, reads=[d_.res, R("cst", 0)], writes=[zres(A_ZB, j)],
                bias=cc(C_LNB + j), scale=cc(C_LNG + j))

        sa = av(A_T, 1024, F32)
        sbb = av(A_T + 1024, 1024, F32)
        rot = Rot(range(8))
        for q8 in range(8):
            s_ga = wload(B_in[20 + q8])
            s_gb = wload(B_in[28 + q8])
            s_oa = wload(B_oa[q8])
            s_ob = wload(B_ob[q8])
            woa = wv(s_oa, 8, 256)
            wob = wv(s_ob, 8, 256)
            bk = {}
            for ocl in range(2):
                bk["ga", ocl] = rot.single()
                inproj_single(s_ga, ocl, bk["ga", ocl])
            for ocl in range(2):
                bk["gb", ocl] = rot.single()
                inproj_single(s_gb, ocl, bk["gb", ocl])
            for ocl in range(2):
                b_ya = bk["ya", ocl] = rot.single()
                for kc in range(WC):
                    mm(psf[:, b_ya, :], woa[:, kc, ocl * 128:(ocl + 1) * 128], za3[:, kc, :], kc == 0, kc == WC - 1,
                       reads=[R("wring", s_oa), zres(A_ZA, kc)], writes=[PS(b_ya)])
            for ocl in range(2):
                b_yb = bk["yb", ocl] = rot.single()
                for kc in range(WC):
                    mm(psf[:, b_yb, :], wob[:, kc, ocl * 128:(ocl + 1) * 128], zb3[:, kc, :], kc == 0, kc == WC - 1,
                       reads=[R("wring", s_ob), zres(A_ZB, kc)], writes=[PS(b_yb)])
            for ocl in range(2):
                oc = q8 * 2 + ocl
                b_ga, b_gb, b_ya, b_yb = bk["ga", ocl], bk["gb", ocl], bk["ya", ocl], bk["yb", ocl]
                act(sa.ap, psf[:, b_ga, :], AF.Sigmoid, reads=[], writes=[PS(b_ga), sa.res])
                act(sbb.ap, psf[:, b_gb, :], AF.Sigmoid, reads=[], writes=[PS(b_gb), sbb.res])
                dve_tt(sa.ap, psf[:, b_ya, :], sa.ap, ALU.mult, reads=[], writes=[PS(b_ya), sa.res])
                dve_tt(sbb.ap, psf[:, b_yb, :], sbb.ap, ALU.mult, reads=[], writes=[PS(b_yb), sbb.res])
                dve_tt(mg3[:, oc, :], sa.ap, sbb.ap, ALU.add, reads=[sa.res, sbb.res],
                       writes=[zres(A_MG, oc)], eng="dve" if ti == 0 else "pool")

        if ti == 0:
            dump("za", za3, [128, WC, 512], BF16, [za.res])
            dump("v", v3, [128, WC, 512], F32, [vbuf.res])
            dump("zb", zb3, [128, WC, 512], BF16, [zb.res])
            dump("mg", mg3, [128, KD, 512], BF16, [mg.res])
        cols = resid_proj(B_o, mg3, lambda kc: zres(A_MG, kc))
        if ti == 0:
            dump("xr1", xr[:], [128, NG, D], F32, [("xr", 0, 16)])
        return cols

    def resid_proj(Bw, actT3, act_res_fn):
        c16 = scol(16)
        cols = [c16 + 4 * g for g in range(NG)]
        crs, sts = [], []
        for cb in range(4):
            base = 0 if cb % 2 == 0 else 4
            for h in range(2):
                slot = wload(Bw[cb][h])
                wvv = wv(slot, 8, 512)
                for g in range(NG):
                    for kl in range(8):
                        kc = h * 8 + kl
                        mm(psf[:, base + g, :], actT3[:, kc, g * 128:(g + 1) * 128], wvv[:, kl, :],
                           kc == 0, kc == KD - 1, reads=[R("wring", slot), act_res_fn(kc)],
                           writes=[PS(base + g)])
            for g in range(NG):
                resid_evac(base + g, g, cb, cols[g])
                if cb == 3 and g % 2 == 1:
                    c2 = rstd_from_parts8(cols[g - 1])
                    crs.extend([c2, c2 + 1])
                    if g == 1:
                        for gg in range(2):
                            sts.append(nt_part1(xr[:, gg, :], ("xr", gg * 4, gg * 4 + 4), 128, crs[gg]))
        return (crs, sts)

    def resid_evac(bank, g, cb, c4):
        dst = xr[:, g, cb * 512:(cb + 1) * 512]
        dve_tt(dst, psf[:, bank, :], dst, ALU.add, reads=[],
               writes=[PS(bank), R("xr", g * 4 + cb)])
        act(junk[:, cb * 512:(cb + 1) * 512], dst, AF.Square, reads=[R("xr", g * 4 + cb)],
            writes=[SM(c4 + cb)], accum_out=small[:, c4 + cb:c4 + cb + 1])

    def norm_xr(gcol, cols):
        crs, sts = cols

        def p1(g):
            return nt_part1(xr[:, g, :], ("xr", g * 4, g * 4 + 4), 128, crs[g])

        def trn(st, g, kc):
            xb, nrows = st
            bank = kc // 2
            o = (kc % 2) * 512 + g * 128
            tr(psb[:, bank, o:o + 128], xs[:, xb, kc * 128:(kc + 1) * 128], ident,
               reads=[R("xs", xb * 2 + kc // 8), R("idn", 0)], writes=[PS(bank)])

        def evac_bank(bank):
            for kc in (2 * bank, 2 * bank + 1):
                src = psb[:, bank, (kc % 2) * 512:(kc % 2) * 512 + 512]
                dst = xnT[:, kc, HALO:HALO + TT]
                if bank % 2 == 0:
                    act(dst, src, AF.Copy, reads=[R("cst", 0)], writes=[PS(bank), R("xnT", kc)],
                        scale=cc(gcol + kc))
                else:
                    ts("dve", dst, src, cc(gcol + kc), None, ALU.mult, None,
                       reads=[R("cst", 0)], writes=[PS(bank), R("xnT", kc)])
        s0, s1 = sts
        for kc in range(KD):
            trn(s0, 0, kc)
        for kc in range(KD):
            trn(s1, 1, kc)
        s2 = p1(2)
        s3 = p1(3)
        for bank in range(8):
            for kc in (2 * bank, 2 * bank + 1):
                trn(s2, 2, kc)
                trn(s3, 3, kc)
            evac_bank(bank)

    def phase2(ti, cols_in):
        norm_xr(C_GX, cols_in)
        qT = av(A_Q, 8192)
        oT = av(A_OT, 8192)
        q3 = qT.ap.rearrange("p (k c) -> p k c", k=KD)
        o3 = oT.ap.rearrange("p (k c) -> p k c", k=KD)
        p_ = av(A_P, 2048, F32)
        pn = av(A_PN, 1024)
        pT = av(A_PT, 4096)
        p3 = p_.ap.rearrange("p (h c) -> p h c", h=NH)
        pn3 = pn.ap.rearrange("p (h c) -> p h c", h=NH)
        pT4 = pT.ap.rearrange("p (h t c) -> p h t c", h=NH, t=2)

        def qres(oc):
            return ("arena", A_Q + oc * 512, A_Q + (oc + 1) * 512)

        def ores(oc):
            return ("arena", A_OT + oc * 512, A_OT + (oc + 1) * 512)

        def pres(h):
            return ("arena", A_P + h * 512, A_P + (h + 1) * 512)

        def pnres(h):
            return ("arena", A_PN + h * 256, A_PN + (h + 1) * 256)

        def ptres(h):
            return ("arena", A_PT + h * 1024, A_PT + (h + 1) * 1024)

        rot = Rot(range(4))

        def stage_q(h):
            for c2 in range(2):
                slot = wload(B_q[h * 2 + c2])
                wvv = wv(slot, 16, 256)
                for ocl in range(2):
                    oc = h * 4 + c2 * 2 + ocl
                    bank = rot.single()
                    for kc in range(KD):
                        mm(psf[:, bank, :], wvv[:, kc, ocl * 128:(ocl + 1) * 128], xnT[:, kc, HALO:HALO + TT],
                           kc == 0, kc == KD - 1, reads=[R("wring", slot), R("xnT", kc)], writes=[PS(bank)])
                    if oc % 2 == 0:
                        act(q3[:, oc, :], psf[:, bank, :], AF.Copy, reads=[], writes=[PS(bank), qres(oc)])
                    else:
                        S.op("dve", lambda e, oc=oc, bank=bank: e.tensor_copy(q3[:, oc, :], psf[:, bank, :]),
                             writes=[PS(bank), qres(oc)])

        def stage_scores(h):
            cm = {}
            for g in range(NG):
                bank = sbank[g]
                for dc in range(4):
                    oc = h * 4 + dc
                    mm(psf[:, bank, 0:NMEM], q3[:, oc, g * 128:(g + 1) * 128], kT[:, oc, :], dc == 0, dc == 3,
                       reads=[qres(oc), R("kT", oc)], writes=[PS(bank)])
            for g in range(NG):
                bank = sbank[g]
                c_mx, c_nb, c_rs, c_ri = scol(), scol(), scol(), scol()
                cm[g] = (c_nb, c_rs, c_ri)
                S.op("dve", lambda e, bank=bank, c=c_mx: e.tensor_reduce(small[:, c:c + 1], psf[:, bank, 0:NMEM],
                                                                          AX.X, ALU.max),
                     writes=[PS(bank), SM(c_mx)])
                ts("dve", small[:, c_nb:c_nb + 1], small[:, c_mx:c_mx + 1], -SCALE, None, ALU.mult, None,
                   reads=[SM(c_mx)], writes=[SM(c_nb)])
            for g in range(NG):
                bank = sbank[g]
                c_nb, c_rs, c_ri = cm[g]
                act(p3[:, g, :], psf[:, bank, 0:NMEM], AF.Exp, reads=[SM(c_nb)],
                    writes=[PS(bank), pres(g), SM(c_rs)], bias=small[:, c_nb:c_nb + 1], scale=SCALE,
                    accum_out=small[:, c_rs:c_rs + 1])
            for g in range(NG):
                c_nb, c_rs, c_ri = cm[g]
                S.op("dve", lambda e, a=c_ri, b=c_rs: e.reciprocal(small[:, a:a + 1], small[:, b:b + 1]),
                     reads=[SM(c_rs)], writes=[SM(c_ri)])
                ts("dve", pn3[:, g, :], p3[:, g, :], small[:, c_ri:c_ri + 1], None, ALU.mult, None,
                   reads=[pres(g), SM(c_ri)], writes=[pnres(g)])

        def stage_tr(h):
            for g in range(NG):
                for t2 in range(2):
                    tr(psb[:, 4 + h % 2, t2 * 512 + g * 128:t2 * 512 + (g + 1) * 128],
                       pn3[:, g, t2 * 128:(t2 + 1) * 128], ident,
                       reads=[pnres(g), R("idn", 0)], writes=[PS(4 + h % 2)])

        def stage_pt(h):
            src = psb[:, 4 + h % 2, :].rearrange("p (t c) -> p t c", t=2)
            if h % 2 == 0:
                act(pT4[:, h, :, :], src, AF.Copy, reads=[], writes=[PS(4 + h % 2), ptres(h)])
            else:
                S.op("dve", lambda e: e.tensor_copy(pT4[:, h, :, :], src), writes=[PS(4 + h % 2), ptres(h)])

        def stage_pv(h):
            for dvc in range(4):
                oc = h * 4 + dvc
                bank = 6 + dvc % 2
                for t2 in range(2):
                    mm(psf[:, bank, :], vm[:, t2, oc * 128:(oc + 1) * 128], pT4[:, h, t2, :], t2 == 0, t2 == 1,
                       reads=[("vm", t2 * 4 + oc // 4, t2 * 4 + oc // 4 + 1), ptres(h)], writes=[PS(bank)])
                if dvc % 2 == 0:
                    act(o3[:, oc, :], psf[:, bank, :], AF.Copy, reads=[], writes=[PS(bank), ores(oc)])
                else:
                    S.op("dve", lambda e, oc=oc, bank=bank: e.tensor_copy(o3[:, oc, :], psf[:, bank, :]),
                         writes=[PS(bank), ores(oc)])

        sbank = [0, 1, 2, 3]
        stage_q(0)
        stage_scores(0)
        stage_q(1)
        stage_tr(0)
        stage_scores(1)
        stage_q(2)
        stage_pt(0)
        stage_tr(1)
        stage_scores(2)
        stage_pv(0)
        stage_q(3)
        stage_pt(1)
        stage_tr(2)
        stage_scores(3)
        stage_pv(1)
        stage_pt(2)
        stage_pv(2)
        stage_tr(3)
        stage_pt(3)
        stage_pv(3)
        if ti == 0:
            dump("qT", q3, [128, KD, 512], BF16, [qT.res])
            dump("oT", o3, [128, KD, 512], BF16, [oT.res])
        cols = resid_proj(B_xo, o3, ores)
        if ti == 0:
            dump("xr2", xr[:], [128, NG, D], F32, [("xr", 0, 16)])
        return cols

    def phase3(ti, cols_in):
        norm_xr(C_GF, cols_in)
        hb = av(A_H, 22528)
        h3 = hb.ap.rearrange("p (k c) -> p k c", k=KF)
        sgs = [av(A_SG, 1024, F32), av(A_SG + 1024, 1024, F32)]

        def hres(f):
            return ("arena", A_H + f * 512, A_H + (f + 1) * 512)

        rot = Rot(range(8))
        for f2 in range(KF // 2):
            s_g = wload(B_gg[f2])
            s_u = wload(B_uu[f2])
            wg = wv(s_g, 16, 256)
            wu = wv(s_u, 16, 256)
            for fl in range(2):
                f = f2 * 2 + fl
                bg = rot.single()
                bu = rot.single()
                for kc in range(KD):
                    mm(psf[:, bg, :], wg[:, kc, fl * 128:(fl + 1) * 128], xnT[:, kc, HALO:HALO + TT], kc == 0,
                       kc == KD - 1, reads=[R("wring", s_g), R("xnT", kc)], writes=[PS(bg)])
                for kc in range(KD):
                    mm(psf[:, bu, :], wu[:, kc, fl * 128:(fl + 1) * 128], xnT[:, kc, HALO:HALO + TT], kc == 0,
                       kc == KD - 1, reads=[R("wring", s_u), R("xnT", kc)], writes=[PS(bu)])
                sgv = sgs[f % 2]
                act(sgv.ap, psf[:, bg, :], AF.Silu, reads=[], writes=[PS(bg), sgv.res])
                dve_tt(h3[:, f, :], psf[:, bu, :], sgv.ap, ALU.mult, reads=[sgv.res], writes=[PS(bu), hres(f)])

        fc16 = scol(16)
        fcols = [fc16 + 4 * g for g in range(NG)]
        do_pre = ti + 1 < NT
        pend = []
        if do_pre:
            pend = [prologue_p1(ti + 1, -1), prologue_p1(ti + 1, 0)]
        nxt = [[1, 2], [3], [], []]
        for cb in range(4):
            for bi, (k0, nk) in enumerate(DN_P):
                slot = wload(B_dn[cb][bi])
                wvv = wv(slot, nk, 512)
                for g in range(NG):
                    for kl in range(nk):
                        kc = k0 + kl
                        mm(psf[:, g, :], h3[:, kc, g * 128:(g + 1) * 128], wvv[:, kl, :], kc == 0, kc == KF - 1,
                           reads=[R("wring", slot), hres(kc)], writes=[PS(g)])
            for g in range(NG):
                resid_evac(g, g, cb, fcols[g])
            if do_pre:
                for st in pend:
                    prologue_p2(st)
                pend = [prologue_p1(ti + 1, grp) for grp in nxt[cb]]

        if ti == 0:
            dump("xr3", xr[:], [128, NG, D], F32, [("xr", 0, 16)])
        def final_group(g):
            c_r = rstd_from_parts4(fcols[g], iters=3)
            s = stage_ptr[0] % 2
            stage_ptr[0] += 1
            S.op("dve", lambda e: e.scalar_tensor_tensor(
                stage[:, s, :], xr[:, g, :], small[:, c_r:c_r + 1], cst[:, C_GFIN:C_GFIN + D], ALU.mult, ALU.mult),
                reads=[("xr", g * 4, g * 4 + 4), SM(c_r), R("cst", 0)], writes=[R("stage", s)])
            S.op("pool", lambda e: e.dma_start(out=y_d[ti, g * 128:(g + 1) * 128, :], in_=stage[:, s, :]),
                 reads=[R("stage", s)], dma=True, semkey="st%d" % s)
        return [lambda g=g: final_group(g) for g in range(NG)]

    for grp in (-1, 0, 1, 2, 3):
        prologue_p2(prologue_p1(0, grp, "sp"))
    kv_phase_both()
    dump("xnT", xnT[:], [128, KD, TW], BF16, [("xnT", 0, 16)])
    dump("kT", kT[:], [128, KD, NMEM], BF16, [("kT", 0, 16)])
    dump("vm", vm[:], [128, 2, D], BF16, [("vm", 0, 8)])
    finals = []
    for ti in range(NT):
        if ti == NPT:
            kv_reload()
        c1 = phase1(ti, finals)
        c2 = phase2(ti, c1)
        finals = phase3(ti, c2)
    for fn in finals:
        fn()

    S.emit_all(nc, stack)
    stack.close()
    return nc


_CACHE = {}


def _host_consts(g_mix, g_xattn, g_ffn, g_mem, conv_a_w, conv_b_w, conv_b_bias, ln_b_g, ln_b_b, g_final):
    c = np.zeros((128, NCST), np.float32)

    def fm(v, n):
        return np.ascontiguousarray(np.asarray(v, np.float32).reshape(n, 128).T)

    c[:, C_GMIX:C_GMIX + 16] = fm(g_mix, 16)
    c[:, C_GX:C_GX + 16] = fm(g_xattn, 16)
    c[:, C_GF:C_GF + 16] = fm(g_ffn, 16)
    c[:, C_GMEM:C_GMEM + 16] = fm(g_mem, 16)
    ca = np.asarray(conv_a_w, np.float32).reshape(3, 8, 128)
    c[:, C_CA:C_CA + 24] = ca.transpose(2, 1, 0).reshape(128, 24)
    cb = np.asarray(conv_b_w, np.float32).reshape(31, 8, 128)
    c[:, C_CB:C_CB + 248] = cb.transpose(2, 1, 0).reshape(128, 248)
    c[:, C_BIAS:C_BIAS + 8] = fm(conv_b_bias, 8)
    c[:, C_LNG:C_LNG + 8] = fm(ln_b_g, 8)
    c[:, C_LNB:C_LNB + 8] = fm(ln_b_b, 8)
    c[:, C_EPS] = EPS
    c[:, C_GFIN:C_GFIN + D] = np.broadcast_to(np.asarray(g_final, np.float32).reshape(1, D), (128, D))
    return c


def _run(inputs, NPT, NST, ncores=8, debug=False):
    key = (NPT, NST, debug)
    if key not in _CACHE:
        _CACHE[key] = build_program(NPT, NST, debug)
    nc = _CACHE[key]
    NT = NPT + NST
    f = lambda a: np.asarray(a, np.float32)
    x_prompt, x_sample = f(inputs["x_prompt"]), f(inputs["x_sample"])
    mem_prompt, mem_sample = f(inputs["mem_prompt"]), f(inputs["mem_sample"])
    cst = _host_consts(inputs["g_mix"], inputs["g_xattn"], inputs["g_ffn"], inputs["g_mem"], inputs["conv_a_w"],
                       inputs["conv_b_w"], inputs["conv_b_bias"], inputs["ln_b_g"], inputs["ln_b_b"],
                       inputs["g_final"])
    idn = np.concatenate([np.eye(128, dtype=np.float32), np.ones((128, 128), np.float32)], axis=1).astype(
        ml_dtypes.bfloat16)
    wts = {
        "w_in": f(inputs["w_in"]).reshape(D, DIN),
        "w_out_a": f(inputs["w_out_a"]).reshape(1024, D),
        "w_out_b": f(inputs["w_out_b"]).reshape(1024, D),
        "w_o": f(inputs["w_o"]).reshape(D, D),
        "w_q": f(inputs["w_q"]).reshape(D, D),
        "w_kv": f(inputs["w_kv"]).reshape(D, 2 * D),
        "w_xo": f(inputs["w_xo"]).reshape(D, D),
        "w_gate_up": f(inputs["w_gate_up"]).reshape(D, 2 * DFF),
        "w_down": f(inputs["w_down"]).reshape(DFF, D),
    }
    SP, SS = NPT * TT, NST * TT
    assert x_prompt.shape == (ncores, SP, D) and x_sample.shape == (1, ncores * SS, D)

    def tiles_with_halo(seq, start, ntiles):
        L = seq.shape[0]
        core = seq[start:start + ntiles * TT].reshape(ntiles, TT, D)
        halo = np.zeros((ntiles, 32, D), np.float32)
        for i in range(ntiles):
            t0 = start + i * TT
            lo = max(t0 - HALO, 0)
            halo[i, HALO - (t0 - lo):HALO] = seq[lo:t0]
            hi = min(t0 + TT + HALO, L)
            halo[i, HALO:HALO + (hi - (t0 + TT))] = seq[t0 + TT:hi]
        return core, halo

    in_maps = []
    for c in range(ncores):
        pc, ph = tiles_with_halo(x_prompt[c], 0, NPT)
        sc, sh = tiles_with_halo(x_sample[0], c * SS, NST)
        m = dict(wts)
        m["xc"] = np.ascontiguousarray(np.concatenate([pc, sc], axis=0))
        m["xh"] = np.ascontiguousarray(np.concatenate([ph, sh], axis=0))
        m["mem"] = np.ascontiguousarray(np.stack([mem_prompt[c], mem_sample[0]], axis=0))
        m["cst"] = cst
        m["idn"] = idn
        in_maps.append(m)
    res = run_bass_kernel_spmd(nc, in_maps, core_ids=list(range(ncores)))
    if debug:
        _CACHE["dbg"] = res.results
    ys = [np.asarray(r["y"], np.float32) for r in res.results]
    y_prompt = np.stack([y[:NPT].reshape(SP, D) for y in ys], axis=0)
    y_sample = np.concatenate([y[NPT:].reshape(SS, D) for y in ys], axis=0)[None]
    return y_prompt, y_sample


def kernel(**inputs):
    return _run(inputs, 8, 2)
```

```python
import contextlib
import numpy as np
import ml_dtypes
import concourse.bass as bass
import concourse.mybir as mybir
from concourse.bass_utils import run_bass_kernel_spmd

F32 = mybir.dt.float32
BF16 = mybir.dt.bfloat16
AF = mybir.ActivationFunctionType
ALU = mybir.AluOpType
AX = mybir.AxisListType

D = 2048
KD = 16
DIN = 9216
WC = 8
DFF = 5632
KF = 44
NMEM = 256
NH = 4
TT = 512
NG = 4
HALO = 16
TW = TT + 2 * HALO
HW2 = TW // 2
EPS = 1e-6
SCALE = 512.0 ** -0.5
NSLOT = 4
SLOT_ELEMS = 4096
NPRE_SEM = 40

C_GMIX, C_GX, C_GF, C_GMEM = 0, 16, 32, 48
C_CA = 64
C_CB = 88
C_BIAS = 336
C_LNG = 344
C_LNB = 352
C_EPS = 360
C_GFIN = 368
NCST = 368 + 2048


class Op:
    __slots__ = ("eng", "emit", "deps", "signal", "count", "dma", "semkey", "semval", "idx", "pos")

    def __init__(self, eng, emit, dma=False, semkey=None):
        self.eng = eng
        self.emit = emit
        self.deps = {}
        self.signal = False
        self.count = 0
        self.dma = dma
        self.semkey = semkey
        self.semval = 0
        self.idx = 0


class Region:
    __slots__ = ("lo", "hi", "writer", "readers")

    def __init__(self, lo, hi, writer):
        self.lo, self.hi, self.writer = lo, hi, writer
        self.readers = {}


class Sched:
    ENGS = ("pe", "act", "dve", "pool", "sp")

    def __init__(self):
        self.ops = {e: [] for e in self.ENGS}
        self.spaces = {}
        self.dma_counts = {}
        self.seq = 0
        self.ins = 0

    def _add_dep(self, op, dep):
        if dep is None or dep is op:
            return
        if (not dep.dma) and dep.eng == op.eng and op.eng == "pe":
            return
        key = ("d", dep.semkey) if dep.dma else ("e", dep.eng)
        cur = op.deps.get(key)
        if cur is None or dep.idx > cur.idx:
            op.deps[key] = dep

    def _access(self, op, res, write):
        space, lo, hi = res
        regs = self.spaces.setdefault(space, [])
        for r in regs:
            if r.lo < hi and lo < r.hi:
                self._add_dep(op, r.writer)
                if write:
                    for o in r.readers.values():
                        self._add_dep(op, o)
        if write:
            regs[:] = [r for r in regs if not (lo <= r.lo and r.hi <= hi)]
            regs.append(Region(lo, hi, op))
        else:
            for r in regs:
                if r.lo == lo and r.hi == hi:
                    break
            else:
                r = Region(lo, hi, None)
                regs.append(r)
            key = ("d", op.semkey, op.idx) if op.dma else op.eng
            r.readers[key] = op

    def op(self, eng, emit, reads=(), writes=(), dma=False, semkey=None, at=None):
        o = Op(eng, emit, dma, semkey)
        self.seq += 1
        o.pos = self.seq
        if dma:
            n = self.dma_counts.get(semkey, 0) + 1
            self.dma_counts[semkey] = n
            o.semval = 16 * n
            o.idx = n
        else:
            o.idx = o.pos
        for r in reads:
            self._access(o, r, False)
        for r in writes:
            self._access(o, r, True)
        if at == "deps" and o.deps:
            self.ins += 1
            o.pos = max(d.pos for d in o.deps.values()) + 0.5 + 1e-6 * (self.ins % 1000)
            if not dma:
                o.idx = o.pos
        self.ops[eng].append(o)
        return o

    def emit_all(self, nc, stack):
        for e in self.ENGS:
            self.ops[e].sort(key=lambda o: o.pos)
        for e in self.ENGS:
            for o in self.ops[e]:
                for d in o.deps.values():
                    d.signal = True
        esem = {}
        for e in ("pe", "act", "dve", "pool"):
            esem[e] = stack.enter_context(nc.semaphore("sem_" + e))
            c = 0
            for o in self.ops[e]:
                if not o.dma and o.signal:
                    c += 1
                    o.count = c
        dsem = {}
        for k in self.dma_counts:
            dsem[k] = stack.enter_context(nc.semaphore("dsem_%s" % (k,)))
        finals = [(dsem[k], 16 * n) for k, n in self.dma_counts.items()]

        def run(ename, e):
            waited = {}
            for o in self.ops[ename]:
                for d in o.deps.values():
                    if d.dma:
                        s, v, key = dsem[d.semkey], d.semval, ("d", d.semkey)
                    else:
                        s, v, key = esem[d.eng], d.count, ("e", d.eng)
                    if waited.get(key, 0) >= v:
                        continue
                    waited[key] = v
                    e.wait_ge(s, v)
                ins = o.emit(e)
                if o.dma:
                    ins.then_inc(dsem[o.semkey], 16)
                elif o.signal:
                    ins.then_inc(esem[ename], 1)
            if ename == "sp":
                for s, v in finals:
                    e.wait_ge(s, v)

        with nc.Block() as block:
            @block.tensor
            def _(e):
                run("pe", e)

            @block.scalar
            def _(e):
                run("act", e)

            @block.vector
            def _(e):
                run("dve", e)

            @block.gpsimd
            def _(e):
                run("pool", e)

            @block.sync
            def _(e):
                run("sp", e)


def build_program(NPT=8, NST=2, debug=False):
    NT = NPT + NST
    nc = bass.Bass("TRN2", target_bir_lowering=False)
    S = Sched()
    stack = contextlib.ExitStack()

    def din(name, shape, dt=F32):
        return nc.dram_tensor(name, list(shape), dt, kind="ExternalInput").ap()

    xc = din("xc", [NT, TT, D])
    xh = din("xh", [NT, 32, D])
    mem = din("mem", [2, NMEM, D])
    cst_d = din("cst", [128, NCST])
    idn_d = din("idn", [128, 256], BF16)
    W = {
        "w_in": din("w_in", [D, DIN]),
        "w_out_a": din("w_out_a", [1024, D]),
        "w_out_b": din("w_out_b", [1024, D]),
        "w_o": din("w_o", [D, D]),
        "w_q": din("w_q", [D, D]),
        "w_kv": din("w_kv", [D, 2 * D]),
        "w_xo": din("w_xo", [D, D]),
        "w_gate_up": din("w_gate_up", [D, 2 * DFF]),
        "w_down": din("w_down", [DFF, D]),
    }
    y_d = nc.dram_tensor("y", [NT, TT, D], F32, kind="ExternalOutput").ap()
    dbg_i = [0]

    def dump(name, ap, shape, dt, reads):
        if not debug:
            return
        t = nc.dram_tensor("dbg_" + name, list(shape), dt, kind="ExternalOutput").ap()
        S.op("pool", lambda e: e.dma_start(out=t, in_=ap), reads=reads, dma=True, semkey="dbg%d" % dbg_i[0])
        dbg_i[0] += 1

    blocks = []

    def addblk(segs):
        nk = segs[0][2]
        tc = sum(s[4] for s in segs)
        assert nk * tc <= SLOT_ELEMS
        blocks.append((segs, nk, tc))
        return len(blocks) - 1

    B_k = [addblk([("w_kv", 0, 16, c * 256, 256)]) for c in range(8)]
    B_v = [[addblk([("w_kv", h * 1024, 8, 2048 + cb * 512, 512)]) for h in range(2)] for cb in range(4)]
    B_in = [addblk([("w_in", 0, 16, c * 256, 256)]) for c in range(36)]
    B_oa = [addblk([("w_out_a", 0, 8, c * 256, 256)]) for c in range(8)]
    B_ob = [addblk([("w_out_b", 0, 8, c * 256, 256)]) for c in range(8)]
    B_o = [[addblk([("w_o", h * 1024, 8, cb * 512, 512)]) for h in range(2)] for cb in range(4)]
    B_q = [addblk([("w_q", 0, 16, c * 256, 256)]) for c in range(8)]
    B_xo = [[addblk([("w_xo", h * 1024, 8, cb * 512, 512)]) for h in range(2)] for cb in range(4)]
    B_gg = [addblk([("w_gate_up", 0, 16, f2 * 256, 256)]) for f2 in range(KF // 2)]
    B_uu = [addblk([("w_gate_up", 0, 16, DFF + f2 * 256, 256)]) for f2 in range(KF // 2)]
    DN_P = [(0, 8), (8, 8), (16, 8), (24, 8), (32, 8), (40, 4)]
    B_dn = [[addblk([("w_down", k0 * 128, nk, cb * 512, 512)]) for (k0, nk) in DN_P] for cb in range(4)]
    NBLK = len(blocks)
    wscr = nc.dram_tensor("wscr", [NBLK, 128, SLOT_ELEMS], BF16, kind="Internal").ap()

    def sb(name, shape, dt):
        return stack.enter_context(nc.sbuf_tensor(name, list(shape), dt))

    xr = sb("xr", [128, NG, D], F32)
    stage = sb("stage", [128, 2, D], F32)
    xs = sb("xs", [128, 2, D], BF16)
    junk = sb("junk", [128, D], BF16)
    xnT = sb("xnT", [128, KD, TW], BF16)
    ARENA_E = 28672
    arena = sb("arena", [128, ARENA_E], BF16)
    wring = sb("wring", [128, NSLOT, SLOT_ELEMS], BF16)
    kT = sb("kT", [128, KD, NMEM], BF16)
    vm = sb("vm", [128, 2, D], BF16)
    cst = sb("cst_sb", [128, NCST], F32)
    idn = sb("idn_sb", [128, 256], BF16)
    dgA = sb("dgA", [128, 24, 128], BF16)
    dgB = sb("dgB", [128, 31, 128], BF16)
    small = sb("small", [128, 256], F32)
    small_i = small[:].bitcast(mybir.dt.int32)
    psf = stack.enter_context(nc.psum_tensor("psf", [128, 8, 512], F32))
    psb = psf[:].bitcast(BF16)
    ident = idn[:, 0:128]
    ones = idn[:, 128:256]

    class AView:
        def __init__(self, off_e, n_e, dt, shape=None):
            self.res = ("arena", off_e, off_e + n_e)
            ap = arena[:, off_e:off_e + n_e]
            if dt == F32:
                ap = ap.bitcast(F32)
            self.ap = ap
            self.n = n_e if dt == BF16 else n_e // 2

    A_ZA = 0
    A_V = 4096
    A_ZB = 12288
    A_MG = 16384
    A_T = 24576
    A_Q = 0
    A_OT = 8192
    A_P = 16384
    A_PN = 18432
    A_PT = 19456
    A_H = 0
    A_SG = 22528
    A_MN = 0

    def av(off, n, dt=BF16):
        return AView(off, n, dt)

    def R(space, lo, hi=None):
        return (space, lo, lo + 1 if hi is None else hi)

    def PS(b, n=1):
        return ("ps", b, b + n)

    def cc(col, n=1):
        return cst[:, col:col + n]

    small_ptr = [0]

    def scol(n=1):
        if small_ptr[0] + n > 250:
            small_ptr[0] = 0
        c = small_ptr[0]
        small_ptr[0] += n
        return c

    def SM(c, n=1):
        return ("small", c, c + n)

    def act(out, in_, func, reads, writes, bias=None, scale=None, accum_out=None):
        kw = {}
        if bias is not None:
            kw["bias"] = bias
        if scale is not None:
            kw["scale"] = scale
        if accum_out is not None:
            kw["accum_out"] = accum_out
        return S.op("act", lambda e: e.activation(out, in_, func, **kw), reads, writes)

    def dve_tt(out, in0, in1, op, reads, writes, eng="dve"):
        return S.op(eng, lambda e: e.tensor_tensor(out, in0, in1, op), reads, writes)

    def ts(eng, out, in0, s1, s2, op0, op1, reads, writes):
        if op1 is None:
            return S.op(eng, lambda e: e.tensor_scalar(out, in0, s1, None, op0), reads, writes)
        return S.op(eng, lambda e: e.tensor_scalar(out, in0, s1, s2, op0, op1), reads, writes)

    def mm(out, lhsT, rhs, start, stop, reads, writes):
        return S.op("pe", lambda e: e.matmul(out, lhsT, rhs, start=start, stop=stop), reads, writes)

    def tr(out, in_, idm, reads, writes):
        return S.op("pe", lambda e: e.transpose(out, in_, idm), reads, writes)


    ws_seq = [0]
    ws_mark = []
    ws_done = set()
    ws_uses = {}
    NKVBLK = 16
    PF = NSLOT - 1

    def wload(b):
        i = ws_seq[0]
        slot = i % NSLOT
        ws_seq[0] += 1
        at = "deps"
        segs, nk, tc = blocks[b]
        n = nk * tc
        if b in ws_done:
            S.op("sp", lambda e: e.dma_start(out=wring[:, slot, 0:n], in_=wscr[b][:, 0:n]),
                 reads=[R("wscr", b)], writes=[R("wring", slot)], dma=True, semkey="w%d" % slot, at=at)
        else:
            ws_uses[b] = ws_uses.get(b, 0) + 1
            do_store = True
            if do_store:
                ws_done.add(b)
            off = 0
            for si, (wn, row0, nk_, col0, ncols) in enumerate(segs):
                src = W[wn][row0:row0 + nk * 128, col0:col0 + ncols].rearrange("(k p) c -> p k c", p=128)
                dst = wring[:, slot, 0:n].rearrange("p (k c) -> p k c", k=nk)[:, :, off:off + ncols]
                S.op("pool", lambda e, src=src, dst=dst: e.dma_start(out=dst, in_=src),
                     writes=[R("wring", slot)],
                     dma=True, semkey="w%d" % slot, at=at)
                off += ncols
            if do_store:
                S.op("sp", lambda e: e.dma_start(out=wscr[b][:, 0:n], in_=wring[:, slot, 0:n]),
                     reads=[R("wring", slot)], writes=[R("wscr", b)], dma=True, semkey="ws%d" % slot)
        return slot

    def wv(slot, nk, tc):
        return wring[:, slot, 0:nk * tc].rearrange("p (k c) -> p k c", k=nk)

    S.op("sp", lambda e: e.dma_start(out=cst[:], in_=cst_d), writes=[R("cst", 0)], dma=True, semkey="c0")
    S.op("sp", lambda e: e.dma_start(out=idn[:], in_=idn_d), writes=[R("idn", 0)], dma=True, semkey="c1")

    for j in range(WC):
        for k in range(3):
            i = j * 3 + k
            ts("pool", dgA[:, i, :], ident, cc(C_CA + i), 0.0, ALU.mult, ALU.add,
               reads=[R("cst", 0), R("idn", 0)], writes=[R("dgA", i)])

    stage_ptr = [0]

    def rstd_cols(c_ss, n, nrows=128, iters=3):
        c_x, c_t, c_y, c_a = scol(n), scol(n), scol(n), scol(n)
        P_ = slice(0, nrows)

        def f(c):
            return small[P_, c:c + n]

        def fi(c):
            return small_i[P_, c:c + n]
        ts("dve", f(c_x), f(c_ss), 1.0 / D, EPS, ALU.mult, ALU.add, reads=[SM(c_ss, n)], writes=[SM(c_x, n)])
        S.op("dve", lambda e: e.tensor_scalar(fi(c_t), fi(c_x), 1, None, ALU.logical_shift_right),
             reads=[SM(c_x, n)], writes=[SM(c_t, n)])
        S.op("dve", lambda e: e.tensor_scalar(fi(c_y), fi(c_t), -1.0, 1597463007.0, ALU.mult, ALU.add),
             reads=[SM(c_t, n)], writes=[SM(c_y, n)])
        for it in range(iters):
            dve_tt(f(c_a), f(c_y), f(c_y), ALU.mult, reads=[SM(c_y, n)], writes=[SM(c_a, n)])
            dve_tt(f(c_a), f(c_a), f(c_x), ALU.mult, reads=[SM(c_x, n)], writes=[SM(c_a, n)])
            ts("dve", f(c_a), f(c_a), -0.5, 1.5, ALU.mult, ALU.add, reads=[], writes=[SM(c_a, n)])
            dve_tt(f(c_y), f(c_y), f(c_a), ALU.mult, reads=[SM(c_a, n)], writes=[SM(c_y, n)])
        return c_y

    def rms_stats(src_ap, src_res, nrows=128):
        c_ss = scol()
        act(junk[0:nrows, :], src_ap, AF.Square, reads=[src_res], writes=[SM(c_ss)],
            accum_out=small[0:nrows, c_ss:c_ss + 1])
        return rstd_cols(c_ss, 1, nrows)

    def rstd_from_parts4(c4, iters=2):
        c_ss = scol()
        S.op("dve", lambda e: e.tensor_reduce(small[:, c_ss:c_ss + 1], small[:, c4:c4 + 4], AX.X, ALU.add),
             reads=[SM(c4, 4)], writes=[SM(c_ss)])
        return rstd_cols(c_ss, 1, iters=iters)

    def rstd_from_parts8(c8, iters=2):
        c_ss = scol(2)
        S.op("dve", lambda e: e.tensor_reduce(small[:, c_ss:c_ss + 2],
                                              small[:, c8:c8 + 8].rearrange("p (g c) -> p g c", g=2),
                                              AX.X, ALU.add),
             reads=[SM(c8, 8)], writes=[SM(c_ss, 2)])
        return rstd_cols(c_ss, 2, iters=iters)

    def rstd_from_parts16(c16):
        c_ss = scol(4)
        S.op("dve", lambda e: e.tensor_reduce(small[:, c_ss:c_ss + 4],
                                              small[:, c16:c16 + 16].rearrange("p (g c) -> p g c", g=4),
                                              AX.X, ALU.add),
             reads=[SM(c16, 16)], writes=[SM(c_ss, 4)])
        return rstd_cols(c_ss, 4)

    tp_ptr = [0]
    xs_ptr = [0]

    def nt_part1(src_ap, src_res, nrows, c_r=None):
        if c_r is None:
            c_r = rms_stats(src_ap, src_res, nrows)
        xb = xs_ptr[0] % 2
        xs_ptr[0] += 1
        HD2 = D // 2
        act(xs[0:nrows, xb, 0:HD2], src_ap[:, 0:HD2], AF.Copy, reads=[src_res, SM(c_r)],
            writes=[R("xs", xb * 2)], scale=small[0:nrows, c_r:c_r + 1])
        ts("dve", xs[0:nrows, xb, HD2:D], src_ap[:, HD2:D], small[0:nrows, c_r:c_r + 1], None, ALU.mult, None,
           reads=[src_res, SM(c_r)], writes=[R("xs", xb * 2 + 1)])
        return (xb, nrows)

    def nt_tr(st, pairs=(4, 6)):
        xb, nrows = st
        pb = pairs[tp_ptr[0] % len(pairs)]
        tp_ptr[0] += 1
        for kc in range(KD):
            bank = pb + kc // 8
            o = (kc % 8) * 128
            tr(psb[:, bank, o:o + nrows], xs[0:nrows, xb, kc * 128:(kc + 1) * 128], ident[0:nrows, 0:nrows],
               reads=[R("xs", xb * 2 + kc // 8), R("idn", 0)], writes=[PS(bank)])
        return pb

    def nt_ev(pb, gcol, dstT, dst_res_fn, col_fn):
        order = [(k, "act") for k in range(0, 6)] + [(k, "dve") for k in range(8, 16)] + \
                [(k, "dve") for k in range(6, 8)]
        for kc, eng in order:
            bank = pb + kc // 8
            o = (kc % 8) * 128
            for (r0, n, dc) in col_fn():
                src = psb[:, bank, o + r0:o + r0 + n]
                dst = dstT[:, kc, dc:dc + n]
                if eng == "act":
                    act(dst, src, AF.Copy, reads=[R("cst", 0)], writes=[PS(bank), dst_res_fn(kc)],
                        scale=cc(gcol + kc))
                else:
                    ts("dve", dst, src, cc(gcol + kc), None, ALU.mult, None,
                       reads=[R("cst", 0)], writes=[PS(bank), dst_res_fn(kc)])

    def nt_part2(st, gcol, dstT, dst_res_fn, col_fn, pairs=(4, 6)):
        pb = nt_tr(st, pairs)
        nt_ev(pb, gcol, dstT, dst_res_fn, col_fn)

    def norm_transpose(src_ap, src_res, nrows, gcol, dstT, dst_res_fn, col_fn, c_r=None, pairs=(4, 6)):
        st = nt_part1(src_ap, src_res, nrows, c_r)
        nt_part2(st, gcol, dstT, dst_res_fn, col_fn, pairs)

    def load_stage(src_dram, nrows, queue="pool"):
        s = stage_ptr[0] % 2
        stage_ptr[0] += 1
        S.op(queue, lambda e: e.dma_start(out=stage[0:nrows, s, :], in_=src_dram),
             writes=[R("stage", s)], dma=True, semkey="ld%d" % s)
        return s

    def prologue_p1(ti, grp, queue="pool"):
        if grp < 0:
            s = load_stage(xh[ti], 32, queue)
            return (grp, nt_part1(stage[0:32, s, :], R("stage", s), 32))
        s = load_stage(xc[ti, grp * 128:(grp + 1) * 128, :], 128, queue)
        return (grp, nt_part1(stage[:, s, :], R("stage", s), 128))

    def prologue_p2(st):
        grp, st1 = st
        if grp < 0:
            nt_part2(st1, C_GMIX, xnT, lambda kc: R("xnT", kc), lambda: [(0, 16, 0), (16, 16, HALO + TT)])
        else:
            nt_part2(st1, C_GMIX, xnT, lambda kc: R("xnT", kc), lambda: [(0, 128, HALO + grp * 128)])

    def prologue_group(ti, grp):
        prologue_p2(prologue_p1(ti, grp))

    kvscr_k = nc.dram_tensor("kvscr_k", [128, KD * NMEM], BF16, kind="Internal").ap()
    kvscr_v = nc.dram_tensor("kvscr_v", [128, 2 * D], BF16, kind="Internal").ap()

    def kv_phase_both(between=()):
        mnTs = [av(A_MN, 4096), av(A_MN + 4096, 4096)]
        mn3s = [m.ap.rearrange("p (k c) -> p k c", k=KD) for m in mnTs]
        k1 = av(A_MN + 8192, 4096)
        v1 = av(A_MN + 12288, 4096)
        k13 = k1.ap.rearrange("p (k c) -> p k c", k=KD)
        v13 = v1.ap.rearrange("p (t c) -> p t c", t=2)

        class _T:
            def __init__(self, m):
                self.m = m

            def __getitem__(self, idx):
                return self.m[idx]
        for ms in range(2):
            for g2 in range(2):
                s = load_stage(mem[ms, g2 * 128:(g2 + 1) * 128, :], 128, "sp")
                norm_transpose(stage[:, s, :], R("stage", s), 128, C_GMEM, _T(mn3s[ms]),
                               lambda kc, ms=ms: mnTs[ms].res, lambda g2=g2: [(0, 128, g2 * 128)])
        bank_i = 0
        between = list(between)
        for c in range(8):
            if c >= 1 and between:
                between.pop(0)()
            slot = wload(B_k[c])
            wvv = wv(slot, 16, 256)
            for ocl in range(2):
                oc = c * 2 + ocl
                for ms in range(2):
                    bank = bank_i % 4
                    bank_i += 1
                    for kc in range(KD):
                        mm(psf[:, bank, 0:NMEM], wvv[:, kc, ocl * 128:(ocl + 1) * 128], mn3s[ms][:, kc, :],
                           kc == 0, kc == KD - 1, reads=[R("wring", slot), mnTs[ms].res], writes=[PS(bank)])
                    dst = kT[:, oc, :] if ms == 0 else k13[:, oc, :]
                    wres = R("kT", oc) if ms == 0 else k1.res
                    if (oc + ms) % 2 == 0:
                        act(dst, psf[:, bank, 0:NMEM], AF.Copy, reads=[], writes=[PS(bank), wres])
                    else:
                        S.op("dve", lambda e, dst=dst, bank=bank: e.tensor_copy(dst, psf[:, bank, 0:NMEM]),
                             writes=[PS(bank), wres])
        for cb in range(4):
            base = (cb % 2) * 4
            for h in range(2):
                slot = wload(B_v[cb][h])
                wvv = wv(slot, 8, 512)
                for ms in range(2):
                    for g2 in range(2):
                        bank = base + ms * 2 + g2
                        for kl in range(8):
                            kc = h * 8 + kl
                            mm(psf[:, bank, :], mn3s[ms][:, kc, g2 * 128:(g2 + 1) * 128], wvv[:, kl, :],
                               kc == 0, kc == KD - 1, reads=[R("wring", slot), mnTs[ms].res], writes=[PS(bank)])
            for ms in range(2):
                for g2 in range(2):
                    bank = base + ms * 2 + g2
                    if ms == 0:
                        dst, wres = vm[:, g2, cb * 512:(cb + 1) * 512], R("vm", g2 * 4 + cb)
                    else:
                        dst, wres = v13[:, g2, cb * 512:(cb + 1) * 512], v1.res
                    if g2 == 0:
                        act(dst, psf[:, bank, :], AF.Copy, reads=[], writes=[PS(bank), wres])
                    else:
                        S.op("dve", lambda e, dst=dst, bank=bank: e.tensor_copy(dst, psf[:, bank, :]),
                             writes=[PS(bank), wres])
        S.op("sp", lambda e: e.dma_start(out=kvscr_k, in_=k1.ap), reads=[k1.res], writes=[R("kvscr", 0)],
             dma=True, semkey="kvs0")
        S.op("sp", lambda e: e.dma_start(out=kvscr_v, in_=v1.ap), reads=[v1.res], writes=[R("kvscr", 1)],
             dma=True, semkey="kvs1")

    def kv_reload():
        S.op("pool", lambda e: e.dma_start(out=kT[:].rearrange("p k c -> p (k c)"), in_=kvscr_k),
             reads=[R("kvscr", 0)], writes=[("kT", 0, 16)], dma=True, semkey="kvl0")
        S.op("pool", lambda e: e.dma_start(out=vm[:].rearrange("p t c -> p (t c)"), in_=kvscr_v),
             reads=[R("kvscr", 1)], writes=[("vm", 0, 8)], dma=True, semkey="kvl1")

    class Rot:
        def __init__(self, banks):
            self.banks = list(banks)
            self.i = 0

        def single(self):
            b = self.banks[self.i % len(self.banks)]
            self.i += 1
            return b

        def pair(self):
            while self.banks[self.i % len(self.banks)] % 2 != 0:
                self.i += 1
            b = self.banks[self.i % len(self.banks)]
            self.i += 2
            return b

    def xn_pair_view(kc):
        return xnT[:, kc, :].rearrange("p (a c) -> p a c", a=2)

    def inproj_pair(slot, ocl, bank):
        wvv = wv(slot, 16, 256)
        for half in range(2):
            for kc in range(KD):
                mm(psf[:, bank + half, 0:HW2], wvv[:, kc, ocl * 128:(ocl + 1) * 128],
                   xnT[:, kc, half * HW2:(half + 1) * HW2], kc == 0, kc == KD - 1,
                   reads=[R("wring", slot), R("xnT", kc)], writes=[PS(bank + half)])

    def inproj_single(slot, ocl, bank):
        wvv = wv(slot, 16, 256)
        for kc in range(KD):
            mm(psf[:, bank, :], wvv[:, kc, ocl * 128:(ocl + 1) * 128], xnT[:, kc, HALO:HALO + TT],
               kc == 0, kc == KD - 1, reads=[R("wring", slot), R("xnT", kc)], writes=[PS(bank)])

    def ps2(bank):
        return psf[:, bank:bank + 2, 0:HW2]

    def v2(ap):
        return ap.rearrange("p (a c) -> p a c", a=2)

    def phase1(ti, finals):
        za = av(A_ZA, 4096)
        vbuf = av(A_V, 8192, F32)
        zb = av(A_ZB, 4096)
        mg = av(A_MG, 8192)
        za3 = za.ap.rearrange("p (k c) -> p k c", k=WC)
        v3 = vbuf.ap.rearrange("p (k c) -> p k c", k=WC)
        zb3 = zb.ap.rearrange("p (k c) -> p k c", k=WC)
        mg3 = mg.ap.rearrange("p (k c) -> p k c", k=KD)

        def zres(base, j, n=512):
            return ("arena", base + j * n, base + (j + 1) * n)

        acs = av(A_T, 1088, F32)
        tts = [av(A_T + 1088, 544), av(A_T + 1632, 544)]
        cA = av(A_T + 2176, 1024, F32)
        rot = Rot(range(8))
        pendA = []

        def conv_a(j, bb):
            tt_ = tts[j % 2]
            bo = rot.single()
            for k in range(3):
                mm(psf[:, bo, :], dgA[:, j * 3 + k, :], tt_.ap[:, HALO - 1 + k:HALO - 1 + k + TT],
                   k == 0, k == 2, reads=[R("dgA", j * 3 + k), tt_.res], writes=[PS(bo)])
            act(cA.ap, psf[:, bo, :], AF.Copy, reads=[], writes=[PS(bo), cA.res])
            dve_tt(za3[:, j, :], psf[:, bb, :], cA.ap, ALU.mult, reads=[cA.res],
                   writes=[PS(bb), zres(A_ZA, j)])

        for jp in range(4):
            s_c = wload(B_in[4 + jp])
            s_v = wload(B_in[8 + jp])
            s_b = wload(B_in[0 + jp])
            for jl in range(2):
                j = jp * 2 + jl
                tt_ = tts[j % 2]
                bc = rot.pair()
                inproj_pair(s_c, jl, bc)
                act(v2(acs.ap), ps2(bc), AF.Copy, reads=[], writes=[PS(bc, 2), acs.res])
                bv = rot.pair()
                inproj_pair(s_v, jl, bv)
                dve_tt(v2(tt_.ap), ps2(bv), v2(acs.ap), ALU.mult, reads=[acs.res], writes=[PS(bv, 2), tt_.res])
                if pendA:
                    conv_a(*pendA.pop())
                bb = rot.single()
                inproj_single(s_b, jl, bb)
                pendA.append((j, bb))
            if finals:
                finals[jp]()
        conv_a(*pendA.pop())
        S.op("pool", lambda e: e.dma_start(out=xr[:], in_=xc[ti].rearrange("(g p) d -> p g d", p=128)),
             writes=[("xr", 0, 16)], dma=True, semkey="xr")

        sg = av(A_T, 1088, F32)
        us = [av(A_T + 1088, 544), av(A_T + 1632, 544)]
        vbs = [av(A_T + 2176, 512), av(A_ZB, 512)]
        vsqs = [av(A_T + 2688, 512), av(A_ZB + 512, 512)]
        rot = Rot(range(6))
        BS1, BS2 = 6, 7
        pendB = []

        def build_dgb(j):
            for k in range(31):
                if ti == 0:
                    act(dgB[:, k, :], ident, AF.Copy, reads=[R("cst", 0), R("idn", 0)], writes=[R("dgB", k)],
                        scale=cc(C_CB + j * 31 + k))
                else:
                    ts("pool", dgB[:, k, :], ident, cc(C_CB + j * 31 + k), 0.0, ALU.mult, ALU.add,
                       reads=[R("cst", 0), R("idn", 0)], writes=[R("dgB", k)])

        def conv_b(j):
            u_ = us[j % 2]
            vres = ("arena", A_V + j * 1024, A_V + (j + 1) * 1024)
            bo = rot.single()
            for k in range(31):
                mm(psf[:, bo, :], dgB[:, k, :], u_.ap[:, 1 + k:1 + k + TT], k == 0, k == 30,
                   reads=[R("dgB", k), u_.res], writes=[PS(bo)])
            if j + 1 < WC:
                build_dgb(j + 1)
            act(v3[:, j, :], psf[:, bo, :], AF.Identity, reads=[R("cst", 0)],
                writes=[PS(bo), vres], bias=cc(C_BIAS + j))
            vb, vsq = vbs[j % 2], vsqs[j % 2]
            S.op("dve", lambda e: e.tensor_copy(vb.ap, v3[:, j, :]), reads=[vres], writes=[vb.res])
            act(vsq.ap, v3[:, j, :], AF.Square, reads=[vres], writes=[vsq.res])
            if j > 0:
                stats_b(j - 1)

        def stats_b(j):
            vb, vsq = vbs[j % 2], vsqs[j % 2]
            mm(psf[:, BS1, :], ones, vb.ap, j == 0, j == WC - 1, reads=[R("idn", 0), vb.res], writes=[PS(BS1)])
            mm(psf[:, BS2, :], ones, vsq.ap, j == 0, j == WC - 1, reads=[R("idn", 0), vsq.res], writes=[PS(BS2)])

        build_dgb(0)
        for jp in range(4):
            s_g = wload(B_in[16 + jp])
            s_v = wload(B_in[12 + jp])
            for jl in range(2):
                j = jp * 2 + jl
                u_ = us[j % 2]
                bg = rot.pair()
                inproj_pair(s_g, jl, bg)
                act(v2(sg.ap), ps2(bg), AF.Sigmoid, reads=[], writes=[PS(bg, 2), sg.res])
                bv = rot.pair()
                inproj_pair(s_v, jl, bv)
                dve_tt(v2(u_.ap), ps2(bv), v2(sg.ap), ALU.mult, reads=[sg.res], writes=[PS(bv, 2), u_.res])
                if pendB:
                    conv_b(pendB.pop())
                pendB.append(j)
        conv_b(pendB.pop())
        stats_b(WC - 1)

        mean = av(A_T, 1024, F32)
        rstd = av(A_T + 1024, 1024, F32)
        tmp = av(A_T + 2048, 1024, F32)
        dd = av(A_T + 3072, 1024, F32)
        ts("dve", mean.ap, psf[:, BS1, :], 1.0 / 1024, None, ALU.mult, None, reads=[], writes=[PS(BS1), mean.res])
        dve_tt(tmp.ap, mean.ap, mean.ap, ALU.mult, reads=[mean.res], writes=[tmp.res])
        S.op("dve", lambda e: e.scalar_tensor_tensor(rstd.ap, psf[:, BS2, :], 1.0 / 1024, tmp.ap,
                                                      ALU.mult, ALU.subtract),
             reads=[tmp.res], writes=[PS(BS2), rstd.res])
        act(tmp.ap, rstd.ap, AF.Sqrt, reads=[rstd.res], writes=[tmp.res], bias=cc(C_EPS))
        S.op("dve", lambda e: e.reciprocal(rstd.ap, tmp.ap), reads=[tmp.res], writes=[rstd.res])
        dds = [dd, tmp]
        for j in range(WC):
            vres = ("arena", A_V + j * 1024, A_V + (j + 1) * 1024)
            d_ = dds[j % 2]
            eng = "dve" if (j % 2 == 0 or ti == 0) else "pool"
            dve_tt(d_.ap, v3[:, j, :], mean.ap, ALU.subtract, reads=[vres, mean.res], writes=[d_.res], eng=eng)
            dve_tt(d_.ap, d_.ap, rstd.ap, ALU.mult, reads=[rstd.res], writes=[d_.res], eng=eng)
            act(zb3[:, j, :], d_.ap, AF.Silu, reads=[d_.res, R("cst", 0)], writes=[zres(A_ZB, j)],
                bias=cc(C_LNB + j), scale=cc(C_LNG + j))

        sa = av(A_T, 1024, F32)
        sbb = av(A_T + 1024, 1024, F32)
        rot = Rot(range(8))
        for q8 in range(8):
            s_ga = wload(B_in[20 + q8])
            s_gb = wload(B_in[28 + q8])
            s_oa = wload(B_oa[q8])
            s_ob = wload(B_ob[q8])
            woa = wv(s_oa, 8, 256)
            wob = wv(s_ob, 8, 256)
            bk = {}
            for ocl in range(2):
                bk["ga", ocl] = rot.single()
                inproj_single(s_ga, ocl, bk["ga", ocl])
            for ocl in range(2):
                bk["gb", ocl] = rot.single()
                inproj_single(s_gb, ocl, bk["gb", ocl])
            for ocl in range(2):
                b_ya = bk["ya", ocl] = rot.single()
                for kc in range(WC):
                    mm(psf[:, b_ya, :], woa[:, kc, ocl * 128:(ocl + 1) * 128], za3[:, kc, :], kc == 0, kc == WC - 1,
                       reads=[R("wring", s_oa), zres(A_ZA, kc)], writes=[PS(b_ya)])
            for ocl in range(2):
                b_yb = bk["yb", ocl] = rot.single()
                for kc in range(WC):
                    mm(psf[:, b_yb, :], wob[:, kc, ocl * 128:(ocl + 1) * 128], zb3[:, kc, :], kc == 0, kc == WC - 1,
                       reads=[R("wring", s_ob), zres(A_ZB, kc)], writes=[PS(b_yb)])
            for ocl in range(2):
                oc = q8 * 2 + ocl
                b_ga, b_gb, b_ya, b_yb = bk["ga", ocl], bk["gb", ocl], bk["ya", ocl], bk["yb", ocl]
                act(sa.ap, psf[:, b_ga, :], AF.Sigmoid, reads=[], writes=[PS(b_ga), sa.res])
                act(sbb.ap, psf[:, b_gb, :], AF.Sigmoid, reads=[], writes=[PS(b_gb), sbb.res])
                dve_tt(sa.ap, psf[:, b_ya, :], sa.ap, ALU.mult, reads=[], writes=[PS(b_ya), sa.res])
                dve_tt(sbb.ap, psf[:, b_yb, :], sbb.ap, ALU.mult, reads=[], writes=[PS(b_yb), sbb.res])
                dve_tt(mg3[:, oc, :], sa.ap, sbb.ap, ALU.add, reads=[sa.res, sbb.res],
                       writes=[zres(A_MG, oc)], eng="dve" if ti == 0 else "pool")

        if ti == 0:
            dump("za", za3, [128, WC, 512], BF16, [za.res])
            dump("v", v3, [128, WC, 512], F32, [vbuf.res])
            dump("zb", zb3, [128, WC, 512], BF16, [zb.res])
            dump("mg", mg3, [128, KD, 512], BF16, [mg.res])
        cols = resid_proj(B_o, mg3, lambda kc: zres(A_MG, kc))
        if ti == 0:
            dump("xr1", xr[:], [128, NG, D], F32, [("xr", 0, 16)])
        return cols

    def resid_proj(Bw, actT3, act_res_fn):
        c16 = scol(16)
        cols = [c16 + 4 * g for g in range(NG)]
        crs, sts = [], []
        for cb in range(4):
            base = 0 if cb % 2 == 0 else 4
            for h in range(2):
                slot = wload(Bw[cb][h])
                wvv = wv(slot, 8, 512)
                for g in range(NG):
                    for kl in range(8):
                        kc = h * 8 + kl
                        mm(psf[:, base + g, :], actT3[:, kc, g * 128:(g + 1) * 128], wvv[:, kl, :],
                           kc == 0, kc == KD - 1, reads=[R("wring", slot), act_res_fn(kc)],
                           writes=[PS(base + g)])
            for g in range(NG):
                resid_evac(base + g, g, cb, cols[g])
                if cb == 3 and g % 2 == 1:
                    c2 = rstd_from_parts8(cols[g - 1])
                    crs.extend([c2, c2 + 1])
                    if g == 1:
                        for gg in range(2):
                            sts.append(nt_part1(xr[:, gg, :], ("xr", gg * 4, gg * 4 + 4), 128, crs[gg]))
        return (crs, sts)

    def resid_evac(bank, g, cb, c4):
        dst = xr[:, g, cb * 512:(cb + 1) * 512]
        dve_tt(dst, psf[:, bank, :], dst, ALU.add, reads=[],
               writes=[PS(bank), R("xr", g * 4 + cb)])
        act(junk[:, cb * 512:(cb + 1) * 512], dst, AF.Square, reads=[R("xr", g * 4 + cb)],
            writes=[SM(c4 + cb)], accum_out=small[:, c4 + cb:c4 + cb + 1])

    def norm_xr(gcol, cols):
        crs, sts = cols

        def p1(g):
            return nt_part1(xr[:, g, :], ("xr", g * 4, g * 4 + 4), 128, crs[g])

        def trn(st, g, kc):
            xb, nrows = st
            bank = kc // 2
            o = (kc % 2) * 512 + g * 128
            tr(psb[:, bank, o:o + 128], xs[:, xb, kc * 128:(kc + 1) * 128], ident,
               reads=[R("xs", xb * 2 + kc // 8), R("idn", 0)], writes=[PS(bank)])

        def evac_bank(bank):
            for kc in (2 * bank, 2 * bank + 1):
                src = psb[:, bank, (kc % 2) * 512:(kc % 2) * 512 + 512]
                dst = xnT[:, kc, HALO:HALO + TT]
                if bank % 2 == 0:
                    act(dst, src, AF.Copy, reads=[R("cst", 0)], writes=[PS(bank), R("xnT", kc)],
                        scale=cc(gcol + kc))
                else:
                    ts("dve", dst, src, cc(gcol + kc), None, ALU.mult, None,
                       reads=[R("cst", 0)], writes=[PS(bank), R("xnT", kc)])
        s0, s1 = sts
        for kc in range(KD):
            trn(s0, 0, kc)
        for kc in range(KD):
            trn(s1, 1, kc)
        s2 = p1(2)
        s3 = p1(3)
        for bank in range(8):
            for kc in (2 * bank, 2 * bank + 1):
                trn(s2, 2, kc)
                trn(s3, 3, kc)
            evac_bank(bank)

    def phase2(ti, cols_in):
        norm_xr(C_GX, cols_in)
        qT = av(A_Q, 8192)
        oT = av(A_OT, 8192)
        q3 = qT.ap.rearrange("p (k c) -> p k c", k=KD)
        o3 = oT.ap.rearrange("p (k c) -> p k c", k=KD)
        p_ = av(A_P, 2048, F32)
        pn = av(A_PN, 1024)
        pT = av(A_PT, 4096)
        p3 = p_.ap.rearrange("p (h c) -> p h c", h=NH)
        pn3 = pn.ap.rearrange("p (h c) -> p h c", h=NH)
        pT4 = pT.ap.rearrange("p (h t c) -> p h t c", h=NH, t=2)

        def qres(oc):
            return ("arena", A_Q + oc * 512, A_Q + (oc + 1) * 512)

        def ores(oc):
            return ("arena", A_OT + oc * 512, A_OT + (oc + 1) * 512)

        def pres(h):
            return ("arena", A_P + h * 512, A_P + (h + 1) * 512)

        def pnres(h):
            return ("arena", A_PN + h * 256, A_PN + (h + 1) * 256)

        def ptres(h):
            return ("arena", A_PT + h * 1024, A_PT + (h + 1) * 1024)

        rot = Rot(range(4))

        def stage_q(h):
            for c2 in range(2):
                slot = wload(B_q[h * 2 + c2])
                wvv = wv(slot, 16, 256)
                for ocl in range(2):
                    oc = h * 4 + c2 * 2 + ocl
                    bank = rot.single()
                    for kc in range(KD):
                        mm(psf[:, bank, :], wvv[:, kc, ocl * 128:(ocl + 1) * 128], xnT[:, kc, HALO:HALO + TT],
                           kc == 0, kc == KD - 1, reads=[R("wring", slot), R("xnT", kc)], writes=[PS(bank)])
                    if oc % 2 == 0:
                        act(q3[:, oc, :], psf[:, bank, :], AF.Copy, reads=[], writes=[PS(bank), qres(oc)])
                    else:
                        S.op("dve", lambda e, oc=oc, bank=bank: e.tensor_copy(q3[:, oc, :], psf[:, bank, :]),
                             writes=[PS(bank), qres(oc)])

        def stage_scores(h):
            cm = {}
            for g in range(NG):
                bank = sbank[g]
                for dc in range(4):
                    oc = h * 4 + dc
                    mm(psf[:, bank, 0:NMEM], q3[:, oc, g * 128:(g + 1) * 128], kT[:, oc, :], dc == 0, dc == 3,
                       reads=[qres(oc), R("kT", oc)], writes=[PS(bank)])
            for g in range(NG):
                bank = sbank[g]
                c_mx, c_nb, c_rs, c_ri = scol(), scol(), scol(), scol()
                cm[g] = (c_nb, c_rs, c_ri)
                S.op("dve", lambda e, bank=bank, c=c_mx: e.tensor_reduce(small[:, c:c + 1], psf[:, bank, 0:NMEM],
                                                                          AX.X, ALU.max),
                     writes=[PS(bank), SM(c_mx)])
                ts("dve", small[:, c_nb:c_nb + 1], small[:, c_mx:c_mx + 1], -SCALE, None, ALU.mult, None,
                   reads=[SM(c_mx)], writes=[SM(c_nb)])
            for g in range(NG):
                bank = sbank[g]
                c_nb, c_rs, c_ri = cm[g]
                act(p3[:, g, :], psf[:, bank, 0:NMEM], AF.Exp, reads=[SM(c_nb)],
                    writes=[PS(bank), pres(g), SM(c_rs)], bias=small[:, c_nb:c_nb + 1], scale=SCALE,
                    accum_out=small[:, c_rs:c_rs + 1])
            for g in range(NG):
                c_nb, c_rs, c_ri = cm[g]
                S.op("dve", lambda e, a=c_ri, b=c_rs: e.reciprocal(small[:, a:a + 1], small[:, b:b + 1]),
                     reads=[SM(c_rs)], writes=[SM(c_ri)])
                ts("dve", pn3[:, g, :], p3[:, g, :], small[:, c_ri:c_ri + 1], None, ALU.mult, None,
                   reads=[pres(g), SM(c_ri)], writes=[pnres(g)])

        def stage_tr(h):
            for g in range(NG):
                for t2 in range(2):
                    tr(psb[:, 4 + h % 2, t2 * 512 + g * 128:t2 * 512 + (g + 1) * 128],
                       pn3[:, g, t2 * 128:(t2 + 1) * 128], ident,
                       reads=[pnres(g), R("idn", 0)], writes=[PS(4 + h % 2)])

        def stage_pt(h):
            src = psb[:, 4 + h % 2, :].rearrange("p (t c) -> p t c", t=2)
            if h % 2 == 0:
                act(pT4[:, h, :, :], src, AF.Copy, reads=[], writes=[PS(4 + h % 2), ptres(h)])
            else:
                S.op("dve", lambda e: e.tensor_copy(pT4[:, h, :, :], src), writes=[PS(4 + h % 2), ptres(h)])

        def stage_pv(h):
            for dvc in range(4):
                oc = h * 4 + dvc
                bank = 6 + dvc % 2
                for t2 in range(2):
                    mm(psf[:, bank, :], vm[:, t2, oc * 128:(oc + 1) * 128], pT4[:, h, t2, :], t2 == 0, t2 == 1,
                       reads=[("vm", t2 * 4 + oc // 4, t2 * 4 + oc // 4 + 1), ptres(h)], writes=[PS(bank)])
                if dvc % 2 == 0:
                    act(o3[:, oc, :], psf[:, bank, :], AF.Copy, reads=[], writes=[PS(bank), ores(oc)])
                else:
                    S.op("dve", lambda e, oc=oc, bank=bank: e.tensor_copy(o3[:, oc, :], psf[:, bank, :]),
                         writes=[PS(bank), ores(oc)])

        sbank = [0, 1, 2, 3]
        stage_q(0)
        stage_scores(0)
        stage_q(1)
        stage_tr(0)
        stage_scores(1)
        stage_q(2)
        stage_pt(0)
        stage_tr(1)
        stage_scores(2)
        stage_pv(0)
        stage_q(3)
        stage_pt(1)
        stage_tr(2)
        stage_scores(3)
        stage_pv(1)
        stage_pt(2)
        stage_pv(2)
        stage_tr(3)
        stage_pt(3)
        stage_pv(3)
        if ti == 0:
            dump("qT", q3, [128, KD, 512], BF16, [qT.res])
            dump("oT", o3, [128, KD, 512], BF16, [oT.res])
        cols = resid_proj(B_xo, o3, ores)
        if ti == 0:
            dump("xr2", xr[:], [128, NG, D], F32, [("xr", 0, 16)])
        return cols

    def phase3(ti, cols_in):
        norm_xr(C_GF, cols_in)
        hb = av(A_H, 22528)
        h3 = hb.ap.rearrange("p (k c) -> p k c", k=KF)
        sgs = [av(A_SG, 1024, F32), av(A_SG + 1024, 1024, F32)]

        def hres(f):
            return ("arena", A_H + f * 512, A_H + (f + 1) * 512)

        rot = Rot(range(8))
        for f2 in range(KF // 2):
            s_g = wload(B_gg[f2])
            s_u = wload(B_uu[f2])
            wg = wv(s_g, 16, 256)
            wu = wv(s_u, 16, 256)
            for fl in range(2):
                f = f2 * 2 + fl
                bg = rot.single()
                bu = rot.single()
                for kc in range(KD):
                    mm(psf[:, bg, :], wg[:, kc, fl * 128:(fl + 1) * 128], xnT[:, kc, HALO:HALO + TT], kc == 0,
                       kc == KD - 1, reads=[R("wring", s_g), R("xnT", kc)], writes=[PS(bg)])
                for kc in range(KD):
                    mm(psf[:, bu, :], wu[:, kc, fl * 128:(fl + 1) * 128], xnT[:, kc, HALO:HALO + TT], kc == 0,
                       kc == KD - 1, reads=[R("wring", s_u), R("xnT", kc)], writes=[PS(bu)])
                sgv = sgs[f % 2]
                act(sgv.ap, psf[:, bg, :], AF.Silu, reads=[], writes=[PS(bg), sgv.res])
                dve_tt(h3[:, f, :], psf[:, bu, :], sgv.ap, ALU.mult, reads=[sgv.res], writes=[PS(bu), hres(f)])

        fc16 = scol(16)
        fcols = [fc16 + 4 * g for g in range(NG)]
        do_pre = ti + 1 < NT
        pend = []
        if do_pre:
            pend = [prologue_p1(ti + 1, -1), prologue_p1(ti + 1, 0)]
        nxt = [[1, 2], [3], [], []]
        for cb in range(4):
            for bi, (k0, nk) in enumerate(DN_P):
                slot = wload(B_dn[cb][bi])
                wvv = wv(slot, nk, 512)
                for g in range(NG):
                    for kl in range(nk):
                        kc = k0 + kl
                        mm(psf[:, g, :], h3[:, kc, g * 128:(g + 1) * 128], wvv[:, kl, :], kc == 0, kc == KF - 1,
                           reads=[R("wring", slot), hres(kc)], writes=[PS(g)])
            for g in range(NG):
                resid_evac(g, g, cb, fcols[g])
            if do_pre:
                for st in pend:
                    prologue_p2(st)
                pend = [prologue_p1(ti + 1, grp) for grp in nxt[cb]]

        if ti == 0:
            dump("xr3", xr[:], [128, NG, D], F32, [("xr", 0, 16)])
        def final_group(g):
            c_r = rstd_from_parts4(fcols[g], iters=3)
            s = stage_ptr[0] % 2
            stage_ptr[0] += 1
            S.op("dve", lambda e: e.scalar_tensor_tensor(
                stage[:, s, :], xr[:, g, :], small[:, c_r:c_r + 1], cst[:, C_GFIN:C_GFIN + D], ALU.mult, ALU.mult),
                reads=[("xr", g * 4, g * 4 + 4), SM(c_r), R("cst", 0)], writes=[R("stage", s)])
            S.op("pool", lambda e: e.dma_start(out=y_d[ti, g * 128:(g + 1) * 128, :], in_=stage[:, s, :]),
                 reads=[R("stage", s)], dma=True, semkey="st%d" % s)
        return [lambda g=g: final_group(g) for g in range(NG)]

    kv_phase_both([lambda grp=grp: prologue_p2(prologue_p1(0, grp, "sp")) for grp in (-1, 0, 1, 2, 3)])
    dump("xnT", xnT[:], [128, KD, TW], BF16, [("xnT", 0, 16)])
    dump("kT", kT[:], [128, KD, NMEM], BF16, [("kT", 0, 16)])
    dump("vm", vm[:], [128, 2, D], BF16, [("vm", 0, 8)])
    finals = []
    for ti in range(NT):
        if ti == NPT:
            kv_reload()
        c1 = phase1(ti, finals)
        c2 = phase2(ti, c1)
        finals = phase3(ti, c2)
    for fn in finals:
        fn()

    S.emit_all(nc, stack)
    stack.close()
    return nc


_CACHE = {}


def _host_consts(g_mix, g_xattn, g_ffn, g_mem, conv_a_w, conv_b_w, conv_b_bias, ln_b_g, ln_b_b, g_final):
    c = np.zeros((128, NCST), np.float32)

    def fm(v, n):
        return np.ascontiguousarray(np.asarray(v, np.float32).reshape(n, 128).T)

    c[:, C_GMIX:C_GMIX + 16] = fm(g_mix, 16)
    c[:, C_GX:C_GX + 16] = fm(g_xattn, 16)
    c[:, C_GF:C_GF + 16] = fm(g_ffn, 16)
    c[:, C_GMEM:C_GMEM + 16] = fm(g_mem, 16)
    ca = np.asarray(conv_a_w, np.float32).reshape(3, 8, 128)
    c[:, C_CA:C_CA + 24] = ca.transpose(2, 1, 0).reshape(128, 24)
    cb = np.asarray(conv_b_w, np.float32).reshape(31, 8, 128)
    c[:, C_CB:C_CB + 248] = cb.transpose(2, 1, 0).reshape(128, 248)
    c[:, C_BIAS:C_BIAS + 8] = fm(conv_b_bias, 8)
    c[:, C_LNG:C_LNG + 8] = fm(ln_b_g, 8)
    c[:, C_LNB:C_LNB + 8] = fm(ln_b_b, 8)
    c[:, C_EPS] = EPS
    c[:, C_GFIN:C_GFIN + D] = np.broadcast_to(np.asarray(g_final, np.float32).reshape(1, D), (128, D))
    return c


def _run(inputs, NPT, NST, ncores=8, debug=False):
    key = (NPT, NST, debug)
    if key not in _CACHE:
        _CACHE[key] = build_program(NPT, NST, debug)
    nc = _CACHE[key]
    NT = NPT + NST
    f = lambda a: np.asarray(a, np.float32)
    x_prompt, x_sample = f(inputs["x_prompt"]), f(inputs["x_sample"])
    mem_prompt, mem_sample = f(inputs["mem_prompt"]), f(inputs["mem_sample"])
    cst = _host_consts(inputs["g_mix"], inputs["g_xattn"], inputs["g_ffn"], inputs["g_mem"], inputs["conv_a_w"],
                       inputs["conv_b_w"], inputs["conv_b_bias"], inputs["ln_b_g"], inputs["ln_b_b"],
                       inputs["g_final"])
    idn = np.concatenate([np.eye(128, dtype=np.float32), np.ones((128, 128), np.float32)], axis=1).astype(
        ml_dtypes.bfloat16)
    wts = {
        "w_in": f(inputs["w_in"]).reshape(D, DIN),
        "w_out_a": f(inputs["w_out_a"]).reshape(1024, D),
        "w_out_b": f(inputs["w_out_b"]).reshape(1024, D),
        "w_o": f(inputs["w_o"]).reshape(D, D),
        "w_q": f(inputs["w_q"]).reshape(D, D),
        "w_kv": f(inputs["w_kv"]).reshape(D, 2 * D),
        "w_xo": f(inputs["w_xo"]).reshape(D, D),
        "w_gate_up": f(inputs["w_gate_up"]).reshape(D, 2 * DFF),
        "w_down": f(inputs["w_down"]).reshape(DFF, D),
    }
    SP, SS = NPT * TT, NST * TT
    assert x_prompt.shape == (ncores, SP, D) and x_sample.shape == (1, ncores * SS, D)

    def tiles_with_halo(seq, start, ntiles):
        L = seq.shape[0]
        core = seq[start:start + ntiles * TT].reshape(ntiles, TT, D)
        halo = np.zeros((ntiles, 32, D), np.float32)
        for i in range(ntiles):
            t0 = start + i * TT
            lo = max(t0 - HALO, 0)
            halo[i, HALO - (t0 - lo):HALO] = seq[lo:t0]
            hi = min(t0 + TT + HALO, L)
            halo[i, HALO:HALO + (hi - (t0 + TT))] = seq[t0 + TT:hi]
        return core, halo

    in_maps = []
    for c in range(ncores):
        pc, ph = tiles_with_halo(x_prompt[c], 0, NPT)
        sc, sh = tiles_with_halo(x_sample[0], c * SS, NST)
        m = dict(wts)
        m["xc"] = np.ascontiguousarray(np.concatenate([pc, sc], axis=0))
        m["xh"] = np.ascontiguousarray(np.concatenate([ph, sh], axis=0))
        m["mem"] = np.ascontiguousarray(np.stack([mem_prompt[c], mem_sample[0]], axis=0))
        m["cst"] = cst
        m["idn"] = idn
        in_maps.append(m)
    res = run_bass_kernel_spmd(nc, in_maps, core_ids=list(range(ncores)))
    if debug:
        _CACHE["dbg"] = res.results
    ys = [np.asarray(r["y"], np.float32) for r in res.results]
    y_prompt = np.stack([y[:NPT].reshape(SP, D) for y in ys], axis=0)
    y_sample = np.concatenate([y[NPT:].reshape(SS, D) for y in ys], axis=0)[None]
    return y_prompt, y_sample


def kernel(**inputs):
    return _run(inputs, 8, 2)
```
